# Optimizing a Trainium2 kernel written in Bass

```python
import math
import jax, jax.numpy as jnp
from jax import lax
import numpy as np

D_MODEL = 1024
BATCH = 4
SEQ = 4096
DEPTH = 1
DEC_BATCH = 8
DEC_SEQ = 32
PAST_LEN = 2048

CHUNK = 64
WINDOW = 128
WINDOW_CHUNKS = WINDOW // CHUNK

ATTN_HEAD_DIM = 64
ATTN_WIDTH = D_MODEL // 2
ATTN_HEADS = ATTN_WIDTH // ATTN_HEAD_DIM
ATTN_KV_HEADS = 2
ATTN_GROUP = ATTN_HEADS // ATTN_KV_HEADS
KV_WIDTH = ATTN_KV_HEADS * ATTN_HEAD_DIM

RWKV_HEAD_DIM = 64
RWKV_WIDTH = D_MODEL - ATTN_WIDTH
RWKV_HEADS = RWKV_WIDTH // RWKV_HEAD_DIM
DECAY_LORA = 64
AAA_LORA = 64
GATE_LORA = 128
RWKV_PROJ = 3 * RWKV_WIDTH + DECAY_LORA + AAA_LORA + GATE_LORA
RWKV_SPLITS = (RWKV_WIDTH, 2 * RWKV_WIDTH, 3 * RWKV_WIDTH,
               3 * RWKV_WIDTH + DECAY_LORA, 3 * RWKV_WIDTH + DECAY_LORA + AAA_LORA)

MIX_WIDTH = ATTN_WIDTH + RWKV_WIDTH
IN_PROJ = ATTN_WIDTH + 2 * KV_WIDTH + RWKV_PROJ

N_MEM = 256
MEM_HEADS = 4
MEM_HEAD_DIM = 128
MEM_WIDTH = MEM_HEADS * MEM_HEAD_DIM

D_FF = 4 * D_MODEL
REL_BUCKETS = 32
REL_MAX_DIST = 128
NORM_EPS = 1e-6
GN_EPS = 64e-5

kernel_name = 'hymba_swa_sink_rwkv7_stream_step'


def rmsnorm(x, g):
    x32 = x.astype(jnp.float32)
    y = x32 * lax.rsqrt(jnp.mean(x32 * x32, axis=-1, keepdims=True) + NORM_EPS)
    return (y * g.astype(jnp.float32)).astype(x.dtype)


def t5_bucket(rel):
    half = REL_BUCKETS // 2
    max_exact = half // 2
    n = jnp.abs(rel)
    large = max_exact + (jnp.log(jnp.maximum(n, 1).astype(jnp.float32) / max_exact)
                         / math.log(REL_MAX_DIST / max_exact) * (half - max_exact)).astype(jnp.int32)
    large = jnp.minimum(large, half - 1)
    return jnp.where(rel > 0, half, 0) + jnp.where(n < max_exact, n, large)


def rel_bias(rel, table):
    b = table[t5_bucket(rel)]
    return jnp.transpose(b, (2, 0, 1)).reshape(ATTN_KV_HEADS, ATTN_GROUP, rel.shape[0], rel.shape[1])


def sink_attention(q, k, v, bias, valid, sink):
    s = jnp.einsum('bcqhgd,bckhd->bchgqk', q, k).astype(jnp.float32) * (ATTN_HEAD_DIM ** -0.5)
    s = s + bias.astype(jnp.float32)
    s = jnp.where(valid[None, :, None, None, None, :], s, -jnp.inf)
    sk = sink.astype(jnp.float32).reshape(1, 1, ATTN_KV_HEADS, ATTN_GROUP, 1, 1)
    m = jnp.maximum(jnp.max(s, axis=-1, keepdims=True), sk)
    p = jnp.exp(s - m)
    den = jnp.sum(p, axis=-1, keepdims=True) + jnp.exp(sk - m)
    return jnp.einsum('bchgqk,bckhd->bcqhgd', (p / den).astype(v.dtype), v)


def window_attention(q, k, v, k_past, v_past, sink, table):
    B, T = q.shape[0], q.shape[1]
    if k_past is None:
        nc = T // CHUNK
        qb = q.reshape(B, nc, CHUNK, ATTN_KV_HEADS, ATTN_GROUP, ATTN_HEAD_DIM)
        pad = ((0, 0), (WINDOW, 0), (0, 0), (0, 0))
        kp = jnp.pad(k, pad).reshape(B, nc + WINDOW_CHUNKS, CHUNK, ATTN_KV_HEADS, ATTN_HEAD_DIM)
        vp = jnp.pad(v, pad).reshape(B, nc + WINDOW_CHUNKS, CHUNK, ATTN_KV_HEADS, ATTN_HEAD_DIM)
        kb = jnp.concatenate([kp[:, i:i + nc] for i in range(WINDOW_CHUNKS + 1)], axis=2)
        vb = jnp.concatenate([vp[:, i:i + nc] for i in range(WINDOW_CHUNKS + 1)], axis=2)
        n_q, n_k = CHUNK, WINDOW + CHUNK
        valid = (jnp.arange(nc)[:, None] * CHUNK - WINDOW + jnp.arange(n_k)[None, :]) >= 0
        k_buf, v_buf = k[:, -WINDOW:], v[:, -WINDOW:]
    else:
        kc = jnp.concatenate([k_past.astype(k.dtype), k], axis=1)
        vc = jnp.concatenate([v_past.astype(v.dtype), v], axis=1)
        qb = q.reshape(B, 1, T, ATTN_KV_HEADS, ATTN_GROUP, ATTN_HEAD_DIM)
        kb, vb = kc[:, None], vc[:, None]
        n_q, n_k = T, WINDOW + T
        valid = jnp.ones((1, n_k), dtype=bool)
        k_buf, v_buf = kc[:, -WINDOW:], vc[:, -WINDOW:]
    rel = jnp.arange(n_k)[None, :] - WINDOW - jnp.arange(n_q)[:, None]
    o = sink_attention(qb, kb, vb, rel_bias(rel, table), valid, sink)
    return o.reshape(B, T, ATTN_WIDTH), k_buf, v_buf


def wkv7_scan(r, w, k, v, a, b, state0):
    def step(S, inp):
        r_t, w_t, k_t, v_t, a_t, b_t = inp
        sa = jnp.einsum('bhij,bhj->bhi', S, a_t)
        S = S * w_t[:, :, None, :] + sa[..., None] * b_t[:, :, None, :] + v_t[..., None] * k_t[:, :, None, :]
        return S, jnp.einsum('bhij,bhj->bhi', S, r_t)
    xs = tuple(jnp.moveaxis(t.astype(jnp.float32), 1, 0) for t in (r, w, k, v, a, b))
    S, ys = lax.scan(step, state0.astype(jnp.float32), xs)
    return jnp.moveaxis(ys, 0, 1), S


def rwkv7_mixer(zr, shift_prev, wkv0, lw):
    B, T = zr.shape[0], zr.shape[1]
    f32 = jnp.float32
    z_prev = jnp.concatenate([shift_prev.astype(zr.dtype), zr[:, :-1]], axis=1)
    zs = zr + (z_prev - zr) * lw['rwkv_mu']
    r, k, v, wd, ad, gd = jnp.split(zs, RWKV_SPLITS, axis=-1)
    w_log = -jax.nn.softplus(-(lw['rwkv_w0'] + jnp.tanh(wd) @ lw['rwkv_w2'])) - 0.5
    decay = jnp.exp(-jnp.exp(w_log.astype(f32)))
    a = jax.nn.sigmoid(lw['rwkv_a0'] + ad @ lw['rwkv_a2'])
    g = jax.nn.sigmoid(gd) @ lw['rwkv_g2']
    heads = lambda t: t.reshape(B, T, RWKV_HEADS, RWKV_HEAD_DIM)
    kk = heads(k * lw['rwkv_k_k']).astype(f32)
    kk = kk / jnp.maximum(jnp.sqrt(jnp.sum(kk * kk, axis=-1, keepdims=True)), 1e-12)
    k = k * (1.0 + (a - 1.0) * lw['rwkv_k_a'])
    r_h, k_h, v_h, a_h = heads(r).astype(f32), heads(k).astype(f32), heads(v).astype(f32), heads(a).astype(f32)
    y, S = wkv7_scan(r_h, heads(decay), k_h, v_h, -kk, kk * a_h, wkv0)
    mu = jnp.mean(y, axis=-1, keepdims=True)
    var = jnp.mean(jnp.square(y - mu), axis=-1, keepdims=True)
    y = ((y - mu) * lax.rsqrt(var + GN_EPS)).reshape(B, T, RWKV_WIDTH)
    y = y * lw['rwkv_ln_w'].astype(f32) + lw['rwkv_ln_b'].astype(f32)
    bonus = jnp.sum(r_h * k_h * lw['rwkv_r_k'].astype(f32), axis=-1, keepdims=True) * v_h
    y = (y + bonus.reshape(B, T, RWKV_WIDTH)).astype(zr.dtype)
    return y * g, zr[:, -1:], S


def memory_kv(mem, lw):
    B = mem.shape[0]
    mn = rmsnorm(mem, lw['norm_mem_g'])
    mk = (mn @ lw['w_mk']).reshape(B, N_MEM, MEM_HEADS, MEM_HEAD_DIM)
    mv = (mn @ lw['w_mv']).reshape(B, N_MEM, MEM_HEADS, MEM_HEAD_DIM)
    return mk, mv


def memory_cross_attention(h, mem_k, mem_v, lw):
    B, T = h.shape[0], h.shape[1]
    q = (h @ lw['w_cq']).reshape(B, T, MEM_HEADS, MEM_HEAD_DIM)
    s = jnp.einsum('bthd,bmhd->bhtm', q, mem_k.astype(h.dtype)).astype(jnp.float32) * (MEM_HEAD_DIM ** -0.5)
    p = jax.nn.softmax(s, axis=-1).astype(h.dtype)
    o = jnp.einsum('bhtm,bmhd->bthd', p, mem_v.astype(h.dtype)).reshape(B, T, MEM_WIDTH)
    return o @ lw['w_co']


def trunk_layer(x, mem_k, mem_v, k_past, v_past, shift_prev, wkv0, lw, table):
    B, T = x.shape[0], x.shape[1]
    z = rmsnorm(x, lw['norm_mix_g']) @ lw['w_in']
    q, k, v, zr = jnp.split(z, (ATTN_WIDTH, ATTN_WIDTH + KV_WIDTH, ATTN_WIDTH + 2 * KV_WIDTH), axis=-1)
    q = q.reshape(B, T, ATTN_HEADS, ATTN_HEAD_DIM)
    k = k.reshape(B, T, ATTN_KV_HEADS, ATTN_HEAD_DIM)
    v = v.reshape(B, T, ATTN_KV_HEADS, ATTN_HEAD_DIM)
    a_out, k_buf, v_buf = window_attention(q, k, v, k_past, v_past, lw['attn_sink'], table)
    r_out, shift_new, S = rwkv7_mixer(zr, shift_prev, wkv0, lw)
    x = x + jnp.concatenate([a_out, r_out], axis=-1) @ lw['w_out']
    x = x + memory_cross_attention(rmsnorm(x, lw['norm_cross_g']), mem_k, mem_v, lw)
    hm = rmsnorm(x, lw['norm_mlp_g'])
    x = x + jnp.square(jax.nn.relu(hm @ lw['w_up'])) @ lw['w_down']
    return x, k_buf, v_buf, shift_new, S


def setup_inputs(seed: int = 0) -> dict:
    key = jax.random.key(seed)
    ks = iter(jax.random.split(key, 48))
    f32 = jnp.float32

    def nrm(shape, scale):
        return scale * jax.random.normal(next(ks), shape, f32)

    def unif(shape, lo, hi):
        return jax.random.uniform(next(ks), shape, f32, minval=lo, maxval=hi)

    L = DEPTH
    return {
        'x_prompt': nrm((BATCH, SEQ, D_MODEL), 1.0),
        'x_sample': nrm((DEC_BATCH, DEC_SEQ, D_MODEL), 1.0),
        'mem_prompt': nrm((BATCH, N_MEM, D_MODEL), 1.0),
        'cache_attn_k': nrm((L, DEC_BATCH, WINDOW, ATTN_KV_HEADS, ATTN_HEAD_DIM), 1.0),
        'cache_attn_v': nrm((L, DEC_BATCH, WINDOW, ATTN_KV_HEADS, ATTN_HEAD_DIM), 1.0),
        'cache_mem_k': nrm((L, DEC_BATCH, N_MEM, MEM_HEADS, MEM_HEAD_DIM), 1.0),
        'cache_mem_v': nrm((L, DEC_BATCH, N_MEM, MEM_HEADS, MEM_HEAD_DIM), 1.0),
        'state_shift': nrm((L, DEC_BATCH, 1, RWKV_PROJ), 1.0),
        'state_wkv': nrm((L, DEC_BATCH, RWKV_HEADS, RWKV_HEAD_DIM, RWKV_HEAD_DIM), 0.3),
        'norm_mix_g': 1.0 + nrm((L, D_MODEL), 0.02),
        'w_in': nrm((L, D_MODEL, IN_PROJ), D_MODEL ** -0.5),
        'attn_sink': nrm((L, ATTN_HEADS), 1.0),
        'rel_bias_table': nrm((REL_BUCKETS, ATTN_HEADS), 0.5),
        'rwkv_mu': unif((L, RWKV_PROJ), 0.0, 1.0),
        'rwkv_w0': unif((L, RWKV_WIDTH), -6.0, -0.5),
        'rwkv_w2': nrm((L, DECAY_LORA, RWKV_WIDTH), 0.1 * DECAY_LORA ** -0.5),
        'rwkv_a0': nrm((L, RWKV_WIDTH), 0.5),
        'rwkv_a2': nrm((L, AAA_LORA, RWKV_WIDTH), 0.1 * AAA_LORA ** -0.5),
        'rwkv_g2': nrm((L, GATE_LORA, RWKV_WIDTH), GATE_LORA ** -0.5),
        'rwkv_k_k': 0.85 + nrm((L, RWKV_WIDTH), 0.05),
        'rwkv_k_a': 1.0 + nrm((L, RWKV_WIDTH), 0.05),
        'rwkv_r_k': nrm((L, RWKV_HEADS, RWKV_HEAD_DIM), 0.1),
        'rwkv_ln_w': 1.0 + nrm((L, RWKV_WIDTH), 0.02),
        'rwkv_ln_b': nrm((L, RWKV_WIDTH), 0.02),
        'w_out': nrm((L, MIX_WIDTH, D_MODEL), MIX_WIDTH ** -0.5),
        'norm_cross_g': 1.0 + nrm((L, D_MODEL), 0.02),
        'norm_mem_g': 1.0 + nrm((L, D_MODEL), 0.02),
        'w_cq': nrm((L, D_MODEL, MEM_WIDTH), D_MODEL ** -0.5),
        'w_mk': nrm((L, D_MODEL, MEM_WIDTH), D_MODEL ** -0.5),
        'w_mv': nrm((L, D_MODEL, MEM_WIDTH), D_MODEL ** -0.5),
        'w_co': nrm((L, MEM_WIDTH, D_MODEL), MEM_WIDTH ** -0.5),
        'norm_mlp_g': 1.0 + nrm((L, D_MODEL), 0.02),
        'w_up': nrm((L, D_MODEL, D_FF), D_MODEL ** -0.5),
        'w_down': nrm((L, D_FF, D_MODEL), D_FF ** -0.5),
        'norm_final_g': 1.0 + nrm((D_MODEL,), 0.02),
    }


def reference(x_prompt, x_sample, mem_prompt, cache_attn_k, cache_attn_v, cache_mem_k, cache_mem_v,
              state_shift, state_wkv, norm_mix_g, w_in, attn_sink, rel_bias_table, rwkv_mu, rwkv_w0,
              rwkv_w2, rwkv_a0, rwkv_a2, rwkv_g2, rwkv_k_k, rwkv_k_a, rwkv_r_k, rwkv_ln_w, rwkv_ln_b,
              w_out, norm_cross_g, norm_mem_g, w_cq, w_mk, w_mv, w_co, norm_mlp_g, w_up, w_down,
              norm_final_g):
    Bp = x_prompt.shape[0]
    xp, xs = x_prompt, x_sample
    p_k, p_v, p_mk, p_mv, p_sh, p_S = [], [], [], [], [], []
    s_k, s_v, s_sh, s_S = [], [], [], []
    for l in range(DEPTH):
        lw = {
            'norm_mix_g': norm_mix_g[l], 'w_in': w_in[l], 'attn_sink': attn_sink[l],
            'rwkv_mu': rwkv_mu[l], 'rwkv_w0': rwkv_w0[l], 'rwkv_w2': rwkv_w2[l],
            'rwkv_a0': rwkv_a0[l], 'rwkv_a2': rwkv_a2[l], 'rwkv_g2': rwkv_g2[l],
            'rwkv_k_k': rwkv_k_k[l], 'rwkv_k_a': rwkv_k_a[l], 'rwkv_r_k': rwkv_r_k[l],
            'rwkv_ln_w': rwkv_ln_w[l], 'rwkv_ln_b': rwkv_ln_b[l], 'w_out': w_out[l],
            'norm_cross_g': norm_cross_g[l], 'norm_mem_g': norm_mem_g[l],
            'w_cq': w_cq[l], 'w_mk': w_mk[l], 'w_mv': w_mv[l], 'w_co': w_co[l],
            'norm_mlp_g': norm_mlp_g[l], 'w_up': w_up[l], 'w_down': w_down[l],
        }
        mk, mv = memory_kv(mem_prompt, lw)
        shift0 = jnp.zeros((Bp, 1, RWKV_PROJ), xp.dtype)
        wkv_zero = jnp.zeros((Bp, RWKV_HEADS, RWKV_HEAD_DIM, RWKV_HEAD_DIM), jnp.float32)
        xp, kb, vb, sh, S = trunk_layer(xp, mk, mv, None, None, shift0, wkv_zero, lw, rel_bias_table)
        p_k.append(kb); p_v.append(vb); p_mk.append(mk); p_mv.append(mv); p_sh.append(sh); p_S.append(S)
        xs, kb2, vb2, sh2, S2 = trunk_layer(xs, cache_mem_k[l], cache_mem_v[l], cache_attn_k[l], cache_attn_v[l],
                                            state_shift[l], state_wkv[l], lw, rel_bias_table)
        s_k.append(kb2); s_v.append(vb2); s_sh.append(sh2); s_S.append(S2)
    y_prompt = rmsnorm(xp, norm_final_g)
    y_sample = rmsnorm(xs, norm_final_g)
    return (y_prompt, y_sample,
            jnp.stack(p_k), jnp.stack(p_v), jnp.stack(p_mk), jnp.stack(p_mv), jnp.stack(p_sh), jnp.stack(p_S),
            jnp.stack(s_k), jnp.stack(s_v), jnp.stack(s_sh), jnp.stack(s_S))
```

```python
import numpy as np
import concourse.bass as bass
import concourse.mybir as mybir
from concourse.bass_utils import run_bass_kernel_spmd

F32 = mybir.dt.float32
BF16 = mybir.dt.bfloat16
F32R = mybir.dt.float32r
I32 = mybir.dt.int32
ALU = mybir.AluOpType
AF = mybir.ActivationFunctionType
AX = mybir.AxisListType

SAME_ENG_SYNC = True


class Buf:
    __slots__ = ("name", "last_w", "readers", "excl", "pe_last")

    def __init__(self, name):
        self.name = name
        self.excl = False
        self.pe_last = None
        self.last_w = None
        self.readers = []


class V:
    __slots__ = ("ap", "buf")

    def __init__(self, ap, buf):
        self.ap = ap
        self.buf = buf

    def __getitem__(self, k):
        return V(self.ap[k], self.buf)

    def bitcast(self, dt):
        return V(self.ap.bitcast(dt), self.buf)

    def rearrange(self, s, **kw):
        return V(self.ap.rearrange(s, **kw), self.buf)

    def bc(self, shape):
        return V(self.ap.to_broadcast(shape), self.buf)

    def unsq(self, ax):
        return V(self.ap.unsqueeze(ax), self.buf)


class Op:
    __slots__ = ("eng", "idx", "fn", "deps", "needed", "is_dma", "sem", "val", "prev_val", "cnt")

    def __init__(self, eng, idx, fn, is_dma):
        self.eng = eng
        self.idx = idx
        self.fn = fn
        self.deps = []
        self.needed = False
        self.is_dma = is_dma
        self.sem = None
        self.val = 0
        self.prev_val = 0
        self.cnt = 0


ENGS = ["pe", "act", "dve", "pool", "sp"]


class Prog:
    def __init__(self, nc, n_dma_sems=40):
        self.nc = nc
        self.ops = {e: [] for e in ENGS}
        self.known = {e: {} for e in ENGS}
        self.known_dma = {e: set() for e in ENGS}
        self.n_dma_sems = n_dma_sems
        self.dma_uses = [0] * n_dma_sems
        self.dma_rr = 0
        self.dma_rr_sw = 0
        self.n_hw = n_dma_sems - 12
        self.out_dmas = []
        self.bufs = []

    def buf(self, name):
        b = Buf(name)
        self.bufs.append(b)
        return b

    def _emit(self, eng, fn, outs, ins, is_dma=False, force=()):
        op = Op(eng, len(self.ops[eng]), fn, is_dma)
        deps = []
        for v in ins:
            b = v.buf
            if b.last_w is not None:
                deps.append(b.last_w)
            if b.excl:
                deps.extend(r for r in b.readers if r.eng != eng)
        for v in outs:
            b = v.buf
            if b.last_w is not None:
                deps.append(b.last_w)
            deps.extend(b.readers)
        deps.extend(force)
        best = {}
        for d in deps:
            if d is op:
                continue
            if d.is_dma:
                if id(d) in self.known_dma[eng]:
                    continue
                self.known_dma[eng].add(id(d))
                op.deps.append(d)
            else:
                if d.eng == eng and (eng == "pe" or not SAME_ENG_SYNC) and d not in force:
                    continue
                if d.idx <= self.known[eng].get(d.eng, -1):
                    continue
                if d.eng not in best or best[d.eng].idx < d.idx:
                    best[d.eng] = d
        for e2, d in best.items():
            self.known[eng][e2] = d.idx
            d.needed = True
            op.deps.append(d)
        if is_dma:
            if eng == "pool":
                s = self.n_hw + self.dma_rr_sw
                self.dma_rr_sw = (self.dma_rr_sw + 1) % (self.n_dma_sems - self.n_hw)
            else:
                s = self.dma_rr
                self.dma_rr = (self.dma_rr + 1) % self.n_hw
            op.sem = s
            op.prev_val = 16 * self.dma_uses[s]
            self.dma_uses[s] += 1
            op.val = 16 * self.dma_uses[s]
        self.ops[eng].append(op)
        seen = set()
        for v in outs:
            b = v.buf
            if id(b) in seen:
                continue
            seen.add(id(b))
            b.last_w = op
            b.readers = []
        for v in ins:
            b = v.buf
            if id(b) in seen:
                continue
            b.readers.append(op)
        return op

    def dma(self, out, in_, eng="sp", is_output=False, **kw):
        op = self._emit(eng, lambda e: e.dma_start(out=out.ap, in_=in_.ap, **kw), [out], [in_], is_dma=True)
        if is_output:
            self.out_dmas.append(op)
        return op

    def _rg_force(self, out, src):
        bp = src.ap.base_partition
        rg = bp() if callable(bp) else bp
        prev = out.buf.pe_last
        force = ()
        if prev is not None and prev[0] != rg:
            force = (prev[1],)
        return rg, force

    def matmul(self, out, lhsT, rhs, start=True, stop=True, **kw):
        rg, force = self._rg_force(out, lhsT)
        op = self._emit("pe", lambda e: e.matmul(out.ap, lhsT.ap, rhs.ap, start=start, stop=stop, **kw),
                        [out], [lhsT, rhs] + ([] if start else [out]), force=force)
        out.buf.pe_last = (rg, op)
        return op

    def transpose(self, out, in_, ident):
        rg, force = self._rg_force(out, in_)
        op = self._emit("pe", lambda e: e.transpose(out.ap, in_.ap, ident.ap), [out], [in_, ident], force=force)
        out.buf.pe_last = (rg, op)
        return op

    def act(self, out, in_, func, bias=None, scale=None, accum_out=None, eng="act"):
        kw = {}
        ins = [in_]
        outs = [out]
        if bias is not None:
            if isinstance(bias, V):
                kw["bias"] = bias.ap
                ins.append(bias)
            else:
                kw["bias"] = bias
        if scale is not None:
            if isinstance(scale, V):
                kw["scale"] = scale.ap
                ins.append(scale)
            else:
                kw["scale"] = scale
        if accum_out is not None:
            kw["accum_out"] = accum_out.ap
            outs.append(accum_out)
        return self._emit(eng, lambda e: e.activation(out.ap, in_.ap, func, **kw), outs, ins)

    def tt(self, out, in0, in1, op, eng="dve"):
        return self._emit(eng, lambda e: e.tensor_tensor(out.ap, in0.ap, in1.ap, op), [out], [in0, in1])

    def ts(self, out, in0, s1, op0, s2=None, op1=None, accum_out=None, eng="dve"):
        ins = [in0]
        outs = [out]
        a1 = s1.ap if isinstance(s1, V) else s1
        a2 = s2.ap if isinstance(s2, V) else s2
        if isinstance(s1, V):
            ins.append(s1)
        if isinstance(s2, V):
            ins.append(s2)
        kw = {}
        if op1 is not None:
            kw["op1"] = op1
        if accum_out is not None:
            kw["accum_out"] = accum_out.ap
            outs.append(accum_out)
        return self._emit(eng, lambda e: e.tensor_scalar(out.ap, in0.ap, a1, a2, op0, **kw), outs, ins)

    def stt(self, out, in0, scalar, in1, op0, op1, eng="dve"):
        ins = [in0, in1]
        a = scalar.ap if isinstance(scalar, V) else scalar
        if isinstance(scalar, V):
            ins.append(scalar)
        return self._emit(eng, lambda e: e.scalar_tensor_tensor(out.ap, in0.ap, a, in1.ap, op0, op1), [out], ins)

    def copy(self, out, in_, eng="dve"):
        if eng == "act":
            return self._emit("act", lambda e: e.copy(out.ap, in_.ap), [out], [in_])
        return self._emit(eng, lambda e: e.tensor_copy(out.ap, in_.ap), [out], [in_])

    def memset(self, out, val, eng="pool"):
        return self._emit(eng, lambda e: e.memset(out.ap, val), [out], [])

    def reduce(self, out, in_, op, axis=AX.X, eng="dve"):
        return self._emit(eng, lambda e: e.tensor_reduce(out.ap, in_.ap, axis, op), [out], [in_])

    def scan(self, out, d0, d1, initial, op0, op1):
        return self._emit("dve", lambda e: e.tensor_tensor_scan(out.ap, d0.ap, d1.ap, initial, op0, op1),
                          [out], [d0, d1])

    def generic(self, eng, fn, outs, ins):
        return self._emit(eng, fn, outs, ins)

    def barrier(self):
        lasts = {}
        for e in ENGS:
            for op in reversed(self.ops[e]):
                if not op.is_dma and op.fn is not None:
                    lasts[e] = op
                    break
        dma_waits = [(s_, 16 * u) for s_, u in enumerate(self.dma_uses) if u > 0]
        for e in ENGS:
            op = Op(e, len(self.ops[e]), None, False)
            for e2, d in lasts.items():
                if e2 == e:
                    continue
                if d.idx <= self.known[e].get(e2, -1):
                    continue
                self.known[e][e2] = d.idx
                d.needed = True
                op.deps.append(d)
            op.val = dma_waits
            self.ops[e].append(op)

    def finalize(self):
        nc = self.nc
        final_deps = [d for d in self.out_dmas]
        import contextlib
        with contextlib.ExitStack() as st:
            esems = {e: st.enter_context(nc.semaphore("s_" + e)) for e in ENGS}
            dsems = [st.enter_context(nc.semaphore("d%d" % i)) for i in range(self.n_dma_sems)]
            for e in ENGS:
                c = 0
                for op in self.ops[e]:
                    if op.is_dma:
                        continue
                    if op.needed:
                        c += 1
                    op.cnt = c
            block = st.enter_context(nc.Block())

            def run(e, eng_obj):
                for op in self.ops[e]:
                    for d in op.deps:
                        if d.is_dma:
                            eng_obj.wait_ge(dsems[d.sem], d.val)
                        else:
                            eng_obj.wait_ge(esems[d.eng], d.cnt)
                    if op.fn is None:
                        for s_, v_ in op.val:
                            eng_obj.wait_ge(dsems[s_], v_)
                        continue
                    if op.is_dma and op.prev_val > 0:
                        eng_obj.wait_ge(dsems[op.sem], op.prev_val)
                    ins = op.fn(eng_obj)
                    if op.is_dma:
                        ins.then_inc(dsems[op.sem], 16)
                    elif op.needed:
                        ins.then_inc(esems[e], 1)
                if e == "sp":
                    for d in final_deps:
                        eng_obj.wait_ge(dsems[d.sem], d.val)

            @block.sync
            def _(eng):
                run("sp", eng)

            @block.tensor
            def _(eng):
                run("pe", eng)

            @block.scalar
            def _(eng):
                run("act", eng)

            @block.vector
            def _(eng):
                run("dve", eng)

            @block.gpsimd
            def _(eng):
                run("pool", eng)


import contextlib
import math

NTOK = 2048
NT = 256
C0 = math.exp(-0.5)
NEG = -30000.0


class _Stop(Exception):
    pass


STOP = 99
SEQ_HALVES = False
NO_FILL = False


def build_program(nc):
    try:
        _build_program(nc)
    except _Stop:
        pass
    return nc


def _build_program(nc):
    P = Prog(nc)

    def chk(stage):
        if STOP == stage:
            P.finalize()
            raise _Stop()

    st = contextlib.ExitStack()
    D = {}

    def din(name, shape, dt=F32):
        D[name] = V(nc.dram_tensor(name, list(shape), dt, kind="ExternalInput").ap(), P.buf(name))

    def dout(name, shape, dt=F32):
        D[name] = V(nc.dram_tensor(name, list(shape), dt, kind="ExternalOutput").ap(), P.buf(name))

    def dscr(name, shape, dt=F32):
        D[name] = V(nc.dram_tensor(name, list(shape), dt, kind="Internal").ap(), P.buf(name))

    def sb(name, shape, dt=F32, stack=None):
        t = (stack or st).enter_context(nc.sbuf_tensor("s_" + name, list(shape), dt))
        return V(t[tuple(slice(None) for _ in shape)], P.buf(name))

    def ps(name, shape, dt=F32):
        t = st.enter_context(nc.psum_tensor("p_" + name, list(shape), dt))
        b = P.buf(name)
        b.excl = True
        return V(t[tuple(slice(None) for _ in shape)], b)

    for n, s in [("xp", (NTOK, 1024)), ("xprev", (NTOK, 1024)), ("xs", (32, 1024)), ("mem", (256, 1024)),
                 ("ck", (128, 128)), ("cv", (128, 128)), ("cmk", (256, 512)), ("cmv", (256, 512)),
                 ("sshT", (128, 14)), ("swkv", (8, 64, 64)),
                 ("w_in", (1024, 2560)), ("w_out", (1024, 1024)), ("w_cq", (1024, 512)), ("w_mk", (1024, 512)),
                 ("w_mv", (1024, 512)), ("w_co", (512, 1024)), ("w_up", (1024, 4096)), ("w_down", (4096, 1024)),
                 ("w2a2", (128, 512)), ("g2", (128, 512)),
                 ("gvec", (128, 4, 8)), ("gfin", (1, 1024)), ("mu", (128, 14)), ("pvec", (128, 7, 4)),
                 ("sink", (8, 1)), ("table", (32, 8)),
                 ("c_ident", (128, 128)), ("c_J", (128, 128)), ("c_bones", (128, 128)), ("c_bones64", (128, 128)),
                 ("c_ident2", (128, 64)), ("c_mar", (64, 2, 64)), ("c_mts", (64, 64)), ("c_rmask", (128, NT)),
                 ("c_maskadd", (128, 2, 128)), ("c_oh", (32, 384)), ("negmask", (128, 1))]:
        din(n, s)
    for n, s in [("yp", (NTOK, 1024)), ("ys", (32, 1024)), ("pk", (128, 128)), ("pv", (128, 128)),
                 ("pmk", (256, 512)), ("pmv", (256, 512)), ("psh", (14, 128)), ("pwkv", (64, 8, 64)),
                 ("sk", (128, 128)), ("sv", (128, 128)), ("ssh", (14, 128)), ("swkvo", (64, 8, 64))]:
        dout(n, s)
    dscr("x2s", (NTOK + 32, 1024))
    dscr("tbd", (8, 384))
    dscr("wupb", (1024, 4096), BF16)
    dscr("wdnb", (4096, 1024), BF16)

    with st:
        ph1 = contextlib.ExitStack()
        st.enter_context(ph1)
        GA0 = ps("GA0", [128, 512]); GA1 = ps("GA1", [128, 512]); GB = ps("GB", [128, 512])
        GT = ps("GT", [128, 1024], BF16)
        B4 = ps("B4", [128, 512]); B5 = ps("B5", [128, 512]); B6 = ps("B6", [128, 512]); B7 = ps("B7", [128, 512])
        gemm_banks = [GA0, GA1, GB]
        gemm_rr = [0]

        def gbank():
            b = gemm_banks[gemm_rr[0] % 3]
            gemm_rr[0] += 1
            return b

        ident = sb("ident", [128, 128]); P.dma(ident, D["c_ident"])
        Jm = sb("Jm", [128, 128]); P.dma(Jm, D["c_J"])
        bones = sb("bones", [128, 128]); P.dma(bones, D["c_bones"])
        bones64 = sb("bones64", [128, 128]); P.dma(bones64, D["c_bones64"])
        ident2 = sb("ident2", [128, 64]); P.dma(ident2, D["c_ident2"])
        mar = sb("mar", [64, 2, 64]); P.dma(mar, D["c_mar"])
        mts = sb("mts", [64, 64]); P.dma(mts, D["c_mts"])
        rmask = sb("rmask", [128, NT]); P.dma(rmask, D["c_rmask"])
        maskadd = sb("maskadd", [128, 2, 128]); P.dma(maskadd, D["c_maskadd"])
        negmask = sb("negmask", [128, 1]); P.dma(negmask, D["negmask"])
        identb = sb("identb", [128, 128], BF16); P.copy(identb, ident)
        identr = sb("identr", [128, 128], F32R); P.copy(identr, ident)
        onesb = sb("onesb", [128, 128], BF16); P.memset(onesb, 1.0)
        gvec = sb("gvec", [128, 4, 8]); P.dma(gvec, D["gvec"])
        mu = sb("mu", [128, 14]); P.dma(mu, D["mu"])
        pvec = sb("pvec", [128, 7, 4]); P.dma(pvec, D["pvec"])
        sink = sb("sink", [8, 1]); P.dma(sink, D["sink"])
        eps_t = sb("eps_t", [128, 1]); P.memset(eps_t, 0.0)

        w_in = sb("w_in", [128, 8, 2560], BF16, ph1)
        w2a2 = sb("w2a2", [128, 512], BF16, ph1)
        g2 = sb("g2", [128, 512], BF16, ph1)
        w_out = sb("w_out", [128, 8, 1024], BF16, ph1)
        w_cq = sb("w_cq", [128, 8, 512], BF16, ph1)
        w_co = sb("w_co", [128, 4, 1024], BF16, ph1)
        xt = [sb("xt%d" % i, [128, 1024], F32, ph1) for i in range(2)]
        xnT = sb("xnT", [128, 8, NT], BF16, ph1)
        xsc = sb("xsc", [128, 1024], BF16, ph1)
        sq_junk = xsc
        ss = sb("ss", [128, 1], F32, ph1); rstd = sb("rstd", [128, 1], F32, ph1)
        hcT = xnT
        mkT = sb("mkT", [128, 4, 256], BF16, ph1)
        mvt = sb("mvt", [128, 2, 512], BF16, ph1)
        s_sb = sb("s_sb", [128, 2, 4, 128], F32, ph1)
        mtmp = s_sb.rearrange("p a m q -> p (a m q)")[:, 0:512]
        stg = s_sb.rearrange("p a m q -> p (a m q)")[:, 512:1024]
        biasT = sb("biasT", [128, 2, 8, 128], F32, ph1)
        mks = contextlib.ExitStack()
        w_mk = sb("w_mk", [128, 8, 512], BF16, mks); P.dma(w_mk, D["w_mk"].rearrange("(k p) n -> p k n", p=128), eng="pool")
        w_mv = sb("w_mv", [128, 8, 512], BF16, mks); P.dma(w_mv, D["w_mv"].rearrange("(k p) n -> p k n", p=128), eng="pool")
        src = D["w_in"].rearrange("(k p) n -> p k n", p=128)
        for k in range(8):
            P.dma(w_in[:, k, :], src[:, k, :], eng="pool")
        P.dma(w2a2, D["w2a2"], eng="pool")
        P.dma(g2, D["g2"], eng="pool")
        for g in range(2):
            P.dma(w_out[g * 64:(g + 1) * 64, 0:4, :],
                  D["w_out"][g * 256:(g + 1) * 256, :].rearrange("(m d) n -> d m n", d=64), eng="pool")
        P.dma(w_out[:, 4:8, :], D["w_out"][512:1024, :].rearrange("(k p) n -> p k n", p=128), eng="pool")
        P.dma(w_cq, D["w_cq"].rearrange("(k p) n -> p k n", p=128), eng="pool")
        P.dma(w_co, D["w_co"].rearrange("(k p) n -> p k n", p=128), eng="pool")
        bg = []
        for k in range(8):
            bg.append((D["wupb"][k * 128:(k + 1) * 128, :], D["w_up"][k * 128:(k + 1) * 128, :]))
        for k in range(8):
            bg.append((D["wdnb"][k * 512:(k + 1) * 512, :], D["w_down"][k * 512:(k + 1) * 512, :]))

        chk(0)
        with mks as tmp:
            table = sb("table", [32, 8], F32, tmp); P.dma(table, D["table"])
            oh = sb("oh", [32, 384], F32, tmp); P.dma(oh, D["c_oh"])
            tb = sb("tb", [8, 384], F32, tmp)
            P.matmul(GB[0:8, 0:384], table, oh)
            P.ts(tb, GB[0:8, 0:384], sink[0:8, 0:1], ALU.subtract)
            P.dma(D["tbd"], tb)
            XT = sb("XT", [128, 8, 128], F32, tmp)
            for kt in range(2):
                hank = bass.AP(tensor=D["tbd"].ap.tensor, offset=kt * 128, ap=[[1, 128], [384, 8], [1, 128]])
                P.dma(XT, V(hank, D["tbd"].buf))
                for hh in range(8):
                    bk = [GA0, GA1][hh % 2]
                    P.matmul(bk[:, 0:128], XT[:, hh, :], Jm)
                    P.tt(biasT[:, kt, hh, :], bk[:, 0:128], maskadd[:, kt, :], ALU.add)

            def rmsnorm_T(xtile, n, gidx, outT, c0):
                P.memset(ss[0:n, :], 0.0)
                P.act(sq_junk[0:n, :], xtile[0:n, :], AF.Square, accum_out=ss[0:n, :])
                P.ts(ss[0:n, :], ss[0:n, :], 1.0 / 1024.0, ALU.mult, NORM_EPS, ALU.add)
                P.act(ss[0:n, :], ss[0:n, :], AF.Ln)
                P.act(rstd[0:n, :], ss[0:n, :], AF.Exp, scale=-0.5)
                P.act(xsc[0:n, :], xtile[0:n, :], AF.Copy, scale=rstd[0:n, :])
                for k in range(8):
                    P.transpose(GT[:, k * 128:k * 128 + n], xsc[0:n, k * 128:(k + 1) * 128], identb[0:n, 0:n])
                P.tt(outT[:, :, c0:c0 + n], GT.rearrange("p (k t) -> p k t", k=8)[:, :, 0:n],
                     gvec[:, gidx, :].unsq(2).bc([128, 8, n]), ALU.mult)

            def gemm_fm(lhs_w, col0, rhsT, n, evac):
                bk = gbank()
                for k in range(8):
                    P.matmul(bk[:, 0:n], lhs_w[:, k, col0:col0 + 128], rhsT[:, k, 0:n], start=(k == 0), stop=(k == 7))
                evac(bk[:, 0:n])

            chk(1)
            mnT = hcT
            for t in range(2):
                P.dma(xt[t], D["mem"][t * 128:(t + 1) * 128, :])
                rmsnorm_T(xt[t], 128, 2, mnT, t * 128)
            for hd in range(4):
                gemm_fm(w_mk, hd * 128, mnT, 256, lambda src, hd=hd: P.copy(mkT[:, hd, :], src, eng="act"))
            for t in range(2):
                for wi, (wm, oname) in enumerate([(w_mk, "pmk"), (w_mv, "pmv")]):
                    bk = gbank()
                    for k in range(8):
                        P.matmul(bk[:, :], mnT[:, k, t * 128:(t + 1) * 128], wm[:, k, :], start=(k == 0), stop=(k == 7))
                    P.copy(mtmp, bk, eng="act")
                    P.dma(D[oname][t * 128:(t + 1) * 128, :], mtmp, is_output=True)
                    if wi == 1:
                        P.copy(mvt[:, t, :], bk)

            P.barrier()
        qT = sb("qT", [128, 4, NT], BF16, ph1)
        kTh = sb("kTh", [128, 128 + NT], BF16, ph1)
        vth = sb("vth", [128, 3, 128], BF16, ph1)
        kvtok = sb("kvtok", [128, 256], F32, ph1)
        zr = sb("zr", [128, 14, NT + 1], F32, ph1)
        zhist = sb("zhist", [128, 14], F32, ph1)
        dtmp = sb("dtmp", [128, NT], F32, ph1)
        twd = sb("twd", [128, NT], BF16, ph1)
        sgd = sb("sgd", [128, NT], BF16, ph1)
        mixT = sb("mixT", [128, 8, NT], BF16, ph1)
        H = [[sb("H%d_%d" % (ft, i), [128, 64], F32R, ph1) for i in range(2)] for ft in range(4)]
        hpar = [0, 0, 0, 0]

        sw = sb("sw", [128, NT], F32, ph1); cs = sb("cs", [128, NT], F32, ph1)
        Pm2 = [sb("Pm%d" % i, [128, NT], F32, ph1) for i in range(2)]; Pinv = sb("Pinv", [128, NT], F32, ph1)
        Pprev = dtmp; Ee = sb("Ee", [128, NT], F32, ph1)
        asig = sb("asig", [128, NT], F32, ph1)
        kkr = sb("kkr", [128, NT], F32, ph1)
        rn = sb("rn", [128, NT], F32, ph1); kk = rn; kmod = sb("kmod", [128, NT], F32, ph1)
        bb = sb("bb", [128, NT], F32, ph1); t1 = sb("t1", [128, NT], F32, ph1)
        BK2 = [sb("BK%d" % i, [128, 2, NT], F32R, ph1) for i in range(2)]
        AR2 = [sb("AR%d" % i, [128, NT // 32, 2, 32], F32R, ph1) for i in range(2)]
        at2 = [sb("at%d" % i, [128, NT], F32R, ph1) for i in range(2)]
        bh2 = [sb("bh%d" % i, [128, NT], F32R, ph1) for i in range(2)]
        kh2 = [sb("kh%d" % i, [128, NT], F32R, ph1) for i in range(2)]
        bonus2 = [sb("bonus%d" % i, [128, NT], F32, ph1) for i in range(2)]
        yT = sb("yT", [128, NT], F32, ph1)
        vv2 = [sb("vv%d" % i, [128, NT], F32R, ph1) for i in range(2)]
        cen = kkr
        tokm_h = [sb("tokm%d" % i, [64, 2, 3, 128], F32R, ph1) for i in range(2)]
        X0_h = [sb("X0_%d" % i, [64, 2, 2, 128], F32R, ph1) for i in range(2)]
        X1_h = [sb("X1_%d" % i, [64, 2, 2, 128], F32R, ph1) for i in range(2)]
        SM_h = [sb("SM%d" % i, [64, 2, 2, 2, 128], F32R, ph1) for i in range(2)]
        Am_h = [sb("Am%d" % i, [64, 2, 2, 64], F32R, ph1) for i in range(2)]
        ATT_h = [sb("ATT%d" % i, [64, 2, 2, 2, 64], F32R, ph1) for i in range(2)]
        Gbd_h = [sb("Gbd%d" % i, [128, 2, 128], F32R, ph1) for i in range(2)]
        Cst_h = [sb("Cst%d" % i, [128, 2, 64], F32, ph1) for i in range(2)]
        RpT_h = [sb("RpT%d" % i, [128, 2, 64], F32R, ph1) for i in range(2)]
        pT = sb("pT", [128, 2, 4, 128], BF16, ph1)
        rden = sb("rden", [128, 512], F32, ph1)
        x1 = xt
        qcT = qT
        pcT = sb("pcT", [128, 2, NT], BF16, ph1)
        ocT = sb("ocT", [128, 4, NT], BF16, ph1)

        def block(xsrc, n, L, mode, first_pair_mask=False, row0=0, last=False):
            nch = n // L
            nlev = int(math.log2(L))
            ntile = (n + 127) // 128
            for t in range(ntile):
                tn = min(128, n - t * 128)
                P.dma(xt[t][0:tn, :], xsrc[t * 128:t * 128 + tn, :])
                rmsnorm_T(xt[t], tn, 0, xnT, t * 128)
            if mode != "prefix":
                for m in range(4):
                    gemm_fm(w_in, m * 128, xnT, n, lambda src, m=m: P.copy(qT[:, m, 0:n], src, eng="act"))
            gemm_fm(w_in, 512, xnT, n, lambda src: P.copy(kTh[:, 128:128 + n], src, eng="act"))
            for m in range(14):
                gemm_fm(w_in, 768 + m * 128, xnT, n,
                        lambda src, m=m: P.copy(zr[:, m, 1:n + 1], src, eng=("act" if m % 2 else "dve")))
            chk(30)
            for t in range(ntile):
                tn = min(128, n - t * 128)
                bk = gbank()
                for k in range(8):
                    P.matmul(bk[0:tn, 0:256], xnT[:, k, t * 128:t * 128 + tn], w_in[:, k, 512:768],
                             start=(k == 0), stop=(k == 7))
                P.copy(vth[0:tn, 1 + t, :], bk[0:tn, 128:256], eng="act")
                if last and t == ntile - 1:
                    P.copy(kvtok[0:tn, :], bk[0:tn, 0:256])
            chk(31)
            if bg and mode != "sample":
                o_, i_ = bg.pop(0)
                P.dma(o_, i_, eng="pool")
            P.copy(zhist, zr[:, :, n])
            def tshift(m):
                P.tt(dtmp[:, 0:n], zr[:, m, 0:n], zr[:, m, 1:n + 1], ALU.subtract, eng="pool")
                P.stt(zr[:, m, 1:n + 1], dtmp[:, 0:n], mu[:, m:m + 1], zr[:, m, 1:n + 1], ALU.mult, ALU.add)

            tshift(12)
            tshift(13)
            zs = lambda m: zr[:, m, 1:n + 1]
            chk(32)
            P.act(twd[0:64, 0:n], zr[0:64, 12, 1:n + 1], AF.Tanh)
            P.copy(twd[64:128, 0:n], zr[64:128, 12, 1:n + 1], eng="act")
            P.act(sgd[:, 0:n], zs(13), AF.Sigmoid)
            chk(33)
            def _bufs(ft):
                par = ft % 2
                ARv_ = AR2[par].rearrange("p c a t -> p (c a t)")[:, 0:2 * n].rearrange("p (c a t) -> p c a t", a=2, t=L)
                return at2[par], bh2[par], kh2[par], vv2[par], BK2[par], Pm2[par], bonus2[par], ARv_

            def pull(g, k, force=False):
                if g is None or (NO_FILL and not force):
                    return
                for _ in range(k):
                    try:
                        next(g)
                    except StopIteration:
                        return

            def E(ft):
                at_, bh_, kh_, vv_, BK_, Pm_, bonus_, ARv = _bufs(ft)
                GTf = GT.bitcast(F32)
                ebanks = [GTf, GTf]

                def gbank():
                    b = ebanks[ecnt[0] % 2]
                    ecnt[0] += 1
                    return b
                for m_ in (4 + ft, ft, 8 + ft):
                    tshift(m_)
                    yield
                r_, k_, v_ = zs(ft), zs(4 + ft), zs(8 + ft)
                pp = lambda i: pvec[:, i, ft:ft + 1]
                bk = gbank()
                P.matmul(bk[:, 0:n], w2a2[0:64, ft * 128:(ft + 1) * 128], twd[0:64, 0:n])
                P.act(sw[:, 0:n], bk[:, 0:n], AF.Sigmoid, bias=pp(0))
                yield
                bk = gbank()
                P.matmul(bk[:, 0:n], w2a2[64:128, ft * 128:(ft + 1) * 128], twd[64:128, 0:n])
                P.act(asig[:, 0:n], bk[:, 0:n], AF.Sigmoid, bias=pp(1))
                yield
                P.scan(cs[:, 0:n], rmask[:, 0:n], sw[:, 0:n], 0.0, ALU.mult, ALU.add)
                yield
                P.act(Pm_[:, 0:n], cs[:, 0:n], AF.Exp, scale=-C0)
                yield
                P.act(Pinv[:, 0:n], cs[:, 0:n], AF.Exp, scale=C0)
                yield
                P.tt(dtmp[:, 0:n], cs[:, 0:n], sw[:, 0:n], ALU.subtract, eng="pool")
                yield
                P.act(Pprev[:, 0:n], dtmp[:, 0:n], AF.Exp, scale=-C0)
                yield
                csv = cs[:, 0:n].rearrange("p (c t) -> p c t", t=L)
                P.tt(t1[:, 0:n].rearrange("p (c t) -> p c t", t=L), csv[:, :, L - 1:L].bc([128, nch, L]), csv, ALU.subtract)
                yield
                P.act(Ee[:, 0:n], t1[:, 0:n], AF.Exp, scale=-C0)
                yield
                chk(34)
                P.ts(kkr[:, 0:n], k_, pp(2), ALU.mult)
                yield
                P.tt(t1[:, 0:n], kkr[:, 0:n], kkr[:, 0:n], ALU.mult, eng="pool")
                yield
                bk = gbank()
                P.matmul(bk[:, 0:n], bones, t1[:, 0:n])
                P.ts(rn[:, 0:n], bk[:, 0:n], 1e-24, ALU.max)
                yield
                P.act(rn[:, 0:n], rn[:, 0:n], AF.Ln)
                yield
                P.act(rn[:, 0:n], rn[:, 0:n], AF.Exp, scale=-0.5)
                yield
                P.tt(kk[:, 0:n], kkr[:, 0:n], rn[:, 0:n], ALU.mult)
                yield
                P.ts(t1[:, 0:n], asig[:, 0:n], -1.0, ALU.add, pp(3), ALU.mult)
                yield
                P.stt(kmod[:, 0:n], t1[:, 0:n], 1.0, k_, ALU.add, ALU.mult)
                yield
                P.tt(bb[:, 0:n], kk[:, 0:n], asig[:, 0:n], ALU.mult, eng="pool")
                yield
                chk(35)
                nv = lambda a: a[:, 0:n].rearrange("p (c t) -> p c t", t=L)
                P.stt(at_[:, 0:n], kk[:, 0:n], -1.0, Pprev[:, 0:n], ALU.mult, ALU.mult)
                yield
                P.copy(ARv[:, :, 0, :], nv(at_), eng="act")
                yield
                P.tt(ARv[:, :, 1, :], nv(r_) if False else r_.rearrange("p (c t) -> p c t", t=L), nv(Pm_), ALU.mult)
                yield
                P.tt(BK_[:, 0, 0:n], bb[:, 0:n], Pinv[:, 0:n], ALU.mult)
                yield
                P.tt(BK_[:, 1, 0:n], kmod[:, 0:n], Pinv[:, 0:n], ALU.mult, eng="pool")
                yield
                P.tt(bh_[:, 0:n], bb[:, 0:n], Ee[:, 0:n], ALU.mult)
                yield
                P.tt(kh_[:, 0:n], kmod[:, 0:n], Ee[:, 0:n], ALU.mult, eng="pool")
                yield
                if mode != "prefix":
                    P.stt(t1[:, 0:n], r_, pp(6), kmod[:, 0:n], ALU.mult, ALU.mult)
                    bk = gbank()
                    P.matmul(bk[:, 0:n], bones, t1[:, 0:n])
                    P.tt(bonus_[:, 0:n], bk[:, 0:n], v_, ALU.mult)
                chk(36)
                P.copy(vv_[:, 0:n], v_, eng="act")
                yield

            def CG(ft, filler, gnf):
                at_, bh_, kh_, vv_, BK_, Pm_, bonus_, ARv = _bufs(ft)
                r_, k_, v_ = zs(ft), zs(4 + ft), zs(8 + ft)
                pp = lambda i: pvec[:, i, ft:ft + 1]
                Lc = slice(0, L)
                nhv = (nch + 1) // 2
                HB = [[B4, B5, B6], [B7, GA0, GA1]]

                def half(hv):
                    nc2 = min(2, nch - hv * 2)
                    h0, h1, h2 = HB[hv]
                    tk, x0, x1, sm = tokm_h[hv], X0_h[hv], X1_h[hv], SM_h[hv]
                    Amv, ATv, gbd, cst, rp = Am_h[hv], ATT_h[hv], Gbd_h[hv], Cst_h[hv], RpT_h[hv]
                    for ci in range(nc2):
                        c = hv * 2 + ci
                        cc = slice(c * L, (c + 1) * L)
                        tb_ = [h0, h1][ci]
                        for i, srcv in enumerate([at_[:, cc], bh_[:, cc], kh_[:, cc], vv_[:, cc]]):
                            P.transpose(tb_.bitcast(F32R)[Lc, i * 128:(i + 1) * 128], srcv, identr)
                    yield
                    for ci in range(nc2):
                        tb_ = [h0, h1][ci]
                        P.copy(x0[Lc, ci, :, 0:64], tb_[Lc, 0:128].rearrange("p (h j) -> p h j", h=2), eng="act")
                        P.copy(tk[Lc, ci, :, :], tb_[Lc, 128:512].rearrange("p (a j) -> p a j", a=3))
                    yield
                    scb = [h2, h0]
                    for hf in range(2):
                        pr = slice(hf * 64, (hf + 1) * 64)
                        for ci in range(nc2):
                            c = hv * 2 + ci
                            cc = slice(c * L, (c + 1) * L)
                            bkv = scb[hf][Lc, :].rearrange("p (c b x) -> p c b x", c=2, b=2)
                            for b_ in range(2):
                                P.matmul(bkv[:, ci, b_, 0:2 * L], BK_[pr, b_, cc], ARv[pr, c, :, :])
                    for hf in range(2):
                        pr = slice(hf * 64, (hf + 1) * 64)
                        for ci in range(nc2):
                            c = hv * 2 + ci
                            cc = slice(c * L, (c + 1) * L)
                            P.matmul(h1[Lc, (ci * 2 + hf) * 64:(ci * 2 + hf) * 64 + L], ARv[pr, c, 0, :], BK_[pr, 0, cc])
                    yield
                    for hf in range(2):
                        src_ = scb[hf][Lc, :].rearrange("p (c b x) -> p c b x", c=2, b=2)[:, 0:nc2, :, 0:2 * L]
                        for ci in range(nc2):
                            P.tt(sm[Lc, ci, hf, :, 0:2 * L].rearrange("p b (a t) -> p b a t", a=2),
                                 src_[:, ci].rearrange("p b (a t) -> p b a t", a=2),
                                 mar[Lc, :, Lc].unsq(1).bc([L, 2, 2, L]), ALU.mult)
                    NNv = h1[Lc, 0:256].rearrange("p (c h s) -> p c h s", c=2, h=2)[:, 0:nc2, :, Lc]
                    P.tt(Amv[Lc, 0:nc2, :, Lc], NNv, mts[Lc, Lc].unsq(1).unsq(1).bc([L, nc2, 2, L]), ALU.mult)
                    P.copy(ATv[Lc, 0:nc2, :, 0, Lc], sm[Lc, 0:nc2, :, 0, Lc], eng="act")
                    P.tt(ATv[Lc, 0:nc2, :, 1, Lc], sm[Lc, 0:nc2, :, 0, Lc],
                         ident[Lc, Lc].unsq(1).unsq(1).bc([L, nc2, 2, L]), ALU.add)
                    yield
                    for ci in range(nc2):
                        for hf in range(2):
                            P.matmul(h2[Lc, (ci * 2 + hf) * 64:(ci * 2 + hf) * 64 + 64], sm[Lc, ci, hf, 1, Lc],
                                     tk[Lc, ci, 2, hf * 64:(hf + 1) * 64])
                    yield
                    P.copy(x0[Lc, 0:nc2, :, 64:128], h2[Lc, 0:256].rearrange("p (c h x) -> p c h x", c=2, h=2)[:, 0:nc2], eng="act")
                    yield
                    RA = h1[Lc, 0:256].rearrange("p (c h s) -> p c h s", c=2, h=2)
                    RB = h0[Lc, :].rearrange("p (c h a s) -> p c h a s", c=2, h=2, a=2)
                    for r in range(nlev):
                        needA = (r + 1 <= nlev - 1)
                        needAt = (r + 1 <= nlev - 2)
                        needT = (r >= 1)
                        for ci in range(nc2):
                            for hf in range(2):
                                if needA:
                                    P.matmul(RA[:, ci, hf, Lc], ATv[Lc, ci, hf, 0, Lc], Amv[Lc, ci, hf, Lc])
                                if needAt and needT and L == 64:
                                    P.matmul(RB[:, ci, hf, :, Lc], Amv[Lc, ci, hf, Lc], ATv[Lc, ci, hf, :, Lc])
                                else:
                                    if needAt:
                                        P.matmul(RB[:, ci, hf, 0, Lc], Amv[Lc, ci, hf, Lc], ATv[Lc, ci, hf, 0, Lc])
                                    if needT:
                                        P.matmul(RB[:, ci, hf, 1, Lc], Amv[Lc, ci, hf, Lc], ATv[Lc, ci, hf, 1, Lc])
                        yield
                        if needA:
                            P.copy(Amv[Lc, 0:nc2, :, Lc], RA[:, 0:nc2, :, Lc], eng="act")
                        if needAt:
                            P.copy(ATv[Lc, 0:nc2, :, 0, Lc], RB[:, 0:nc2, :, 0, Lc], eng="act")
                        if needT:
                            P.tt(ATv[Lc, 0:nc2, :, 1, Lc], ATv[Lc, 0:nc2, :, 1, Lc], RB[:, 0:nc2, :, 1, Lc], ALU.add)
                        yield
                    for ci in range(nc2):
                        for hf in range(2):
                            P.matmul(h2[Lc, (ci * 2 + hf) * 128:(ci * 2 + hf + 1) * 128], ATv[Lc, ci, hf, 1, Lc], x0[Lc, ci, hf, :])
                    yield
                    x2v = h2[Lc, :].rearrange("p (c h a x) -> p c h a x", c=2, h=2, a=2)
                    x1v = x1[Lc, :, :, :].rearrange("p c a (h x) -> p c a h x", h=2)
                    P.copy(x1v[:, 0:nc2, 0, :, :], x2v[:, 0:nc2, :, 0, :], eng="act")
                    P.copy(x1v[:, 0:nc2, 1, :, :], x2v[:, 0:nc2, :, 1, :])
                    yield
                    GCp = h1[:, :].rearrange("p (c a x) -> p c a x", c=2, a=2)
                    RPp = h0[:, 256:512].rearrange("p (c x) -> p c x", c=2)
                    for ci in range(nc2):
                        P.matmul(GCp[:, ci, 0, :], x1[Lc, ci, 0, :], tk[Lc, ci, 0, :])
                        P.matmul(GCp[:, ci, 1, :], tk[Lc, ci, 0, :], x1[Lc, ci, 1, :], start=True, stop=False)
                        P.matmul(GCp[:, ci, 1, :], tk[Lc, ci, 1, :], tk[Lc, ci, 2, :], start=False, stop=True)
                    if mode != "prefix":
                        for ci in range(nc2):
                            P.matmul(RPp[:, ci, 0:2 * L], x1[Lc, ci, 0, :], sm[Lc, ci, :, 0, L:2 * L])
                    yield
                    PLv = Pm_[:, 0:n].rearrange("p (c t) -> p c t", t=L)[:, hv * 2:hv * 2 + nc2, L - 1:L]
                    for hf in range(2):
                        pr = slice(hf * 64, (hf + 1) * 64)
                        dg = gbd[pr, 0:nc2, hf * 64:(hf + 1) * 64]
                        P.tt(dg, ident2[pr, :].unsq(1).bc([64, nc2, 64]), PLv[pr].bc([64, nc2, 64]), ALU.mult, eng="pool")
                        P.tt(dg, dg, GCp[pr, 0:nc2, 0, hf * 64:(hf + 1) * 64], ALU.add)
                        P.copy(cst[pr, 0:nc2, :], GCp[pr, 0:nc2, 1, hf * 64:(hf + 1) * 64], eng="act")
                        if mode != "prefix":
                            P.tt(rp[pr, 0:nc2, Lc], RPp[pr, 0:nc2, hf * L:(hf + 1) * L],
                                 ARv[pr, hv * 2:hv * 2 + nc2, 1, :], ALU.add)
                    yield
                    Yt = h0[:, 0:256].rearrange("p (c x) -> p c x", c=2)
                    if mode != "prefix":
                        for ci in range(nc2):
                            P.matmul(Yt[:, ci, 0:2 * L], x1[Lc, ci, 1, :], sm[Lc, ci, :, 0, L:2 * L],
                                     start=(ci == 0), stop=False, skip_group_check=True)
                            P.matmul(Yt[:, ci, 0:2 * L], tk[Lc, ci, 2, :], sm[Lc, ci, :, 1, L:2 * L],
                                     start=False, stop=False, skip_group_check=True)
                    yield
                    for ci in range(nc2):
                        Hc = H[ft][hpar[ft]]
                        Hn = H[ft][1 - hpar[ft]]
                        P.matmul(GB[:, 0:64], gbd[:, ci, :], Hc)
                        P.tt(Hn, GB[:, 0:64], cst[:, ci, :], ALU.add)
                        if mode != "prefix":
                            for hf in range(2):
                                pr = slice(hf * 64, (hf + 1) * 64)
                                fx = (lambda v: v.bitcast(F32)) if hf else (lambda v: v)
                                P.matmul(Yt[pr, ci, hf * L:(hf + 1) * L], fx(Hc[pr, :]), fx(rp[pr, ci, Lc]), start=False, stop=True,
                                         skip_group_check=True)
                        hpar[ft] = 1 - hpar[ft]
                        yield
                    if mode != "prefix":
                        for hf in range(2):
                            pr = slice(hf * 64, (hf + 1) * 64)
                            P.copy(yT[pr, hv * 2 * L:(hv * 2 + nc2) * L].rearrange("p (c t) -> p c t", t=L),
                                   Yt[pr, 0:nc2, hf * L:(hf + 1) * L], eng=("act" if hf else "dve"))
                    yield

                gens = [half(hv) for hv in range(nhv)]
                if nhv == 2 and not SEQ_HALVES:
                    gA, gB = gens
                    aliveA = aliveB = True
                    stepA = 0
                    while aliveA or aliveB:
                        if aliveA:
                            try:
                                next(gA)
                            except StopIteration:
                                aliveA = False
                        if aliveB and (stepA >= 3 or not aliveA):
                            try:
                                next(gB)
                            except StopIteration:
                                aliveB = False
                        stepA += 1
                        pull(gnf, 3)
                        pull(filler, 2)
                else:
                    for g_ in gens:
                        for _ in g_:
                            pull(gnf, 3)
                            pull(filler, 2)

            def GN(ft):
                bonus_ = _bufs(ft)[6]
                pp = lambda i: pvec[:, i, ft:ft + 1]
                gb = GT.bitcast(F32)
                sq_ = rden[:, 0:n]
                rn_ = rden[:, 256:256 + n]
                yv = yT[:, 0:n]
                P.matmul(gb[:, 256:256 + n], bones64, yv)
                P.tt(yv, yv, gb[:, 256:256 + n], ALU.subtract)
                yield
                P.act(sq_, yv, AF.Square)
                yield
                P.matmul(gb[:, 256:256 + n], bones64, sq_)
                P.ts(rn_, gb[:, 256:256 + n], GN_EPS, ALU.add)
                yield
                P.act(rn_, rn_, AF.Ln)
                yield
                P.act(rn_, rn_, AF.Exp, scale=-0.5)
                yield
                P.tt(yv, yv, rn_, ALU.mult)
                yield
                P.ts(yv, yv, pp(4), ALU.mult, pp(5), ALU.add)
                yield
                P.tt(yv, yv, bonus_[:, 0:n], ALU.add)
                yield
                P.matmul(gb[:, 256:256 + n], g2[:, ft * 128:(ft + 1) * 128], sgd[:, 0:n])
                P.tt(mixT[:, 4 + ft, 0:n], yv, gb[:, 256:256 + n], ALU.mult)
                yield

            def ATT():
                npair = ntile
                for pi in range(npair):
                    nq = min(128, n - pi * 128)
                    nks = [128, nq]
                    qc = slice(pi * 128, pi * 128 + nq)
                    W = 4 * nq
                    pk3 = lambda v: v.rearrange("p (m q) -> p m q", m=4)
                    for g in range(2):
                        pr = slice(g * 64, (g + 1) * 64)
                        Sv = [B6, B7] if g == 0 else [GA0, GA1]
                        for kt in range(2):
                            nk = nks[kt]
                            k0 = pi * 128 + kt * 128
                            P.matmul(Sv[kt][0:nk, 0:W], kTh[pr, k0:k0 + nk], qT[pr, :, qc])
                            yield
                            s_v = s_sb[0:nk, kt, :, :].rearrange("p m q -> p (m q)")[:, 0:W]
                            p_v = pT[0:nk, kt, :, :].rearrange("p m q -> p (m q)")[:, 0:W]
                            P.stt(pk3(s_v), pk3(Sv[kt][0:nk, 0:W]),
                                  0.125, biasT[0:nk, kt, 4 * g:4 * g + 4, 0:nq], ALU.mult, ALU.add)
                            yield
                            if first_pair_mask and pi == 0 and kt == 0:
                                P.ts(s_v, s_v, negmask[0:nk, 0:1], ALU.add)
                                yield
                            P.act(p_v, s_v, AF.Exp)
                            yield
                        for kt in range(2):
                            nk = nks[kt]
                            p_v = pT[0:nk, kt, :, :].rearrange("p m q -> p (m q)")[:, 0:W]
                            P.matmul(B4[pr, 0:W], onesb[0:nk, 0:64], p_v, start=(kt == 0), stop=(kt == 1))
                            yield
                        for kt in range(2):
                            nk = nks[kt]
                            p_v = pT[0:nk, kt, :, :].rearrange("p m q -> p (m q)")[:, 0:W]
                            P.matmul(B5[pr, 0:W], vth[0:nk, pi + kt, pr], p_v, start=(kt == 0), stop=(kt == 1))
                            yield
                    rv = rden[:, 0:W]
                    P.act(rv, B4[:, 0:W], AF.Ln, bias=1.0)
                    yield
                    P.act(rv, rv, AF.Exp, scale=-1.0)
                    yield
                    P.tt(mixT[:, 0:4, qc], pk3(B5[:, 0:W]), pk3(rv), ALU.mult)
                    yield

            ecnt = [0]
            g0 = E(0)
            attg = ATT() if mode != "prefix" else None
            for _ in range(400):
                pull(g0, 2, True)
                pull(attg, 2, True)
            pull(g0, 10000, True)
            pull(attg, 10000, True)
            gn_prev = None
            for ft in range(4):
                nxt = E(ft + 1) if ft < 3 else None
                CG(ft, nxt, gn_prev)
                pull(gn_prev, 10000, True)
                pull(nxt, 10000, True)
                gn_prev = GN(ft) if mode != "prefix" else None
            pull(gn_prev, 10000, True)
            P.copy(zr[:, :, 0], zhist)
            if mode == "prefix":
                P.copy(kTh[:, 0:128], kTh[:, n:n + 128], eng="pool")
                P.copy(vth[:, 0, :], vth[:, 2, :], eng="pool")
                return
            if mode == "prompt":
                P.copy(kTh[:, 0:128], kTh[:, n:n + 128], eng="pool")
                P.copy(vth[:, 0, :], vth[:, 2, :], eng="pool")
            wo_banks = [[GA0, GA1], [B4, B5]]
            for t in range(ntile):
                tn = min(128, n - t * 128)
                for hh, bk in enumerate(wo_banks[t]):
                    for kc in range(8):
                        P.matmul(bk[0:tn, :], mixT[:, kc, t * 128:t * 128 + tn], w_out[:, kc, hh * 512:(hh + 1) * 512],
                                 start=(kc == 0), stop=(kc == 7))
            for t in range(ntile):
                tn = min(128, n - t * 128)
                for hh, bk in enumerate(wo_banks[t]):
                    P.tt(x1[t][0:tn, hh * 512:(hh + 1) * 512], bk[0:tn, :], xt[t][0:tn, hh * 512:(hh + 1) * 512], ALU.add)
                rmsnorm_T(x1[t], tn, 1, hcT, t * 128)
            for hd in range(4):
                gemm_fm(w_cq, hd * 128, hcT, n, lambda src, hd=hd: P.copy(qcT[:, hd, 0:n], src, eng="act"))
            sc = 128.0 ** -0.5
            SBk = [B4, B7]; DBk = [B5, GB]; PBk = [B6, GA0]
            pc2 = [pcT, pT.rearrange("p a m q -> p (a m q)")[:, 0:2 * NT].rearrange("p (a t) -> p a t", a=2)]
            rd2 = [rden, t1]

            def c_scores(hd):
                for mt in range(2):
                    P.matmul(SBk[hd % 2][:, mt * 256:mt * 256 + n], mkT[:, hd, mt * 128:(mt + 1) * 128], qcT[:, hd, 0:n])

            c_scores(0)
            for hd in range(4):
                if hd + 1 < 4:
                    c_scores(hd + 1)
                pcv = pc2[hd % 2]
                P.act(pcv[:, :, 0:n], SBk[hd % 2][:, :].rearrange("p (a t) -> p a t", a=2)[:, :, 0:n], AF.Exp, scale=sc)
                for mt in range(2):
                    P.matmul(DBk[hd % 2][:, 0:n], onesb, pcv[:, mt, 0:n], start=(mt == 0), stop=(mt == 1))
                for mt in range(2):
                    P.matmul(PBk[hd % 2][:, 0:n], mvt[:, mt, hd * 128:(hd + 1) * 128], pcv[:, mt, 0:n], start=(mt == 0), stop=(mt == 1))
                rdv = rd2[hd % 2]
                P.act(rdv[:, 0:n], DBk[hd % 2][:, 0:n], AF.Ln)
                P.act(rdv[:, 0:n], rdv[:, 0:n], AF.Exp, scale=-1.0)
                P.tt(ocT[:, hd, 0:n], PBk[hd % 2][:, 0:n], rdv[:, 0:n], ALU.mult)
            for t in range(ntile):
                tn = min(128, n - t * 128)
                for hh, bk in enumerate([GA0, GA1]):
                    for hd in range(4):
                        P.matmul(bk[0:tn, :], ocT[:, hd, t * 128:t * 128 + tn], w_co[:, hd, hh * 512:(hh + 1) * 512],
                                 start=(hd == 0), stop=(hd == 3))
                    P.tt(x1[t][0:tn, hh * 512:(hh + 1) * 512], bk[0:tn, :], x1[t][0:tn, hh * 512:(hh + 1) * 512], ALU.add)
                P.dma(D["x2s"][row0 + t * 128:row0 + t * 128 + tn, :], x1[t][0:tn, :])

        def emit_state_outputs(shname, wkvname, kname, vname, n_new):
            P.transpose(GB[0:14, 0:128], zhist, ident)
            P.copy(stg[0:14, 0:128], GB[0:14, 0:128])
            P.dma(D[shname], stg[0:14, 0:128], is_output=True)
            for ft in range(4):
                P.transpose(GB.bitcast(F32R)[0:64, 128:256], H[ft][hpar[ft]], identr)
                P.copy(mtmp[0:64, ft * 128:(ft + 1) * 128], GB[0:64, 128:256])
            P.dma(D[wkvname], mtmp[0:64, :].rearrange("p (h j) -> p h j", h=8), is_output=True)
            P.dma(D[kname][128 - n_new:128, :], kvtok[0:n_new, 0:128], is_output=True)
            P.dma(D[vname][128 - n_new:128, :], kvtok[0:n_new, 128:256], is_output=True)

        chk(2)
        P.memset(zr[:, :, 0], 0.0)
        P.memset(dtmp[:, 0:256], 0.0)
        for ft in range(4):
            P.copy(H[ft][0], dtmp[:, 0:64])
        for i in range(2):
            P.copy(Gbd_h[i].rearrange("p c x -> p (c x)"), dtmp[:, 0:256])
        P.memset(kTh[:, 0:128], 0.0)
        P.memset(vth[:, 0, :], 0.0)
        nblk = NTOK // NT
        for bi in range(nblk):
            block(D["xprev"][bi * NT:(bi + 1) * NT, :], NT, 64, "prefix")
        chk(3)
        for bi in range(nblk):
            block(D["xp"][bi * NT:(bi + 1) * NT, :], NT, 64, "prompt", first_pair_mask=(bi == 0), row0=bi * NT,
                  last=(bi == nblk - 1))
        emit_state_outputs("psh", "pwkv", "pk", "pv", 128)

        chk(4)
        P.dma(zhist, D["sshT"])
        P.copy(zr[:, :, 0], zhist)
        P.dma(mtmp[0:64, :].rearrange("p (h j) -> p h j", h=8), D["swkv"].rearrange("h i j -> i h j"))
        for ft in range(4):
            P.transpose(GB[:, 0:64], mtmp[0:64, ft * 128:(ft + 1) * 128], ident[0:64, 0:64])
            P.copy(H[ft][hpar[ft]], GB[:, 0:64])
        P.dma(stg[:, 0:128], D["ck"]); P.dma(stg[:, 128:256], D["cv"])
        P.transpose(GB[:, 0:128], stg[:, 0:128], ident)
        P.copy(kTh[:, 0:128], GB[:, 0:128])
        P.copy(vth[:, 0, :], stg[:, 128:256])
        for mt in range(2):
            P.dma(mtmp, D["cmk"][mt * 128:(mt + 1) * 128, :])
            for hd in range(4):
                P.transpose(GB[:, 128:256], mtmp[:, hd * 128:(hd + 1) * 128], ident)
                P.copy(mkT[:, hd, mt * 128:(mt + 1) * 128], GB[:, 128:256])
            P.dma(stg, D["cmv"][mt * 128:(mt + 1) * 128, :])
            P.copy(mvt[:, mt, :], stg)
        block(D["xs"], 32, 32, "sample", row0=NTOK, last=True)
        emit_state_outputs("ssh", "swkvo", "sk", "sv", 32)
        P.dma(D["sk"][0:96, :], D["ck"][32:128, :], is_output=True)
        P.dma(D["sv"][0:96, :], D["cv"][32:128, :], is_output=True)

        chk(5)
        while bg:
            o_, i_ = bg.pop(0)
            P.dma(o_, i_, eng="pool")
        P.barrier()
        ph1.close()
        w_up = sb("w_up", [128, 8, 4096], BF16)
        srcu = D["wupb"].rearrange("(k p) n -> p k n", p=128)
        for k in range(8):
            P.dma(w_up[:, k, :], srcu[:, k, :], eng=("sp" if k % 2 == 0 else "act"))
        w_dn = sb("w_dn", [128, 32, 1024], BF16)
        srcd = D["wdnb"].rearrange("(k p) n -> p k n", p=128)
        for k in range(8):
            P.dma(w_dn[:, 4 * k:4 * k + 4, :], srcd[:, 4 * k:4 * k + 4, :], eng=("sp" if k % 2 == 0 else "act"))
        gfin = sb("gfin", [128, 1024]); P.dma(gfin, V(D["gfin"].ap[0:1, :].partition_broadcast(128), D["gfin"].buf))
        x2s_ = [[sb("x2_%d_%d" % (s_, i), [128, 1024]) for i in range(2)] for s_ in range(2)]
        hmT2 = [sb("hmT%d" % s_, [128, 8, NT], BF16) for s_ in range(2)]
        ss3 = sb("ss3", [128, 1]); rs3 = sb("rs3", [128, 1]); sq3 = sb("sq3", [128, 1024])
        h2T = sb("h2T", [128, 32, NT], BF16)
        rl = sb("rl", [128, NT], BF16)
        sq2 = sb("sq2", [128, 1024]); xsc2 = sb("xsc2", [128, 1024], BF16)
        yo = [sb("yo%d" % i, [128, 1024]) for i in range(2)]
        ss2 = sb("ss2", [128, 1]); rs2 = sb("rs2", [128, 1])

        def rmsnorm_T2(xtile, n, outT, c0):
            P.memset(ss2[0:n, :], 0.0)
            P.act(sq2[0:n, :], xtile[0:n, :], AF.Square, accum_out=ss2[0:n, :])
            P.ts(ss2[0:n, :], ss2[0:n, :], 1.0 / 1024.0, ALU.mult, NORM_EPS, ALU.add)
            P.act(ss2[0:n, :], ss2[0:n, :], AF.Ln)
            P.act(rs2[0:n, :], ss2[0:n, :], AF.Exp, scale=-0.5)
            P.act(xsc2[0:n, :], xtile[0:n, :], AF.Copy, scale=rs2[0:n, :])
            for k in range(8):
                P.transpose(GT[:, k * 128:k * 128 + n], xsc2[0:n, k * 128:(k + 1) * 128], identb[0:n, 0:n])
            P.tt(outT[:, :, c0:c0 + n], GT.rearrange("p (k t) -> p k t", k=8)[:, :, 0:n],
                 gvec[:, 3, :].unsq(2).bc([128, 8, n]), ALU.mult)

        def mlp_prep(row0, n, slot):
            ntile = (n + 127) // 128
            for t in range(ntile):
                tn = min(128, n - t * 128)
                P.dma(x2s_[slot][t][0:tn, :], D["x2s"][row0 + t * 128:row0 + t * 128 + tn, :])
                rmsnorm_T2(x2s_[slot][t], tn, hmT2[slot], t * 128)

        def mlp_up(n, slot):
            for f in range(32):
                bk = gbank()
                for k in range(8):
                    P.matmul(bk[:, 0:n], w_up[:, k, f * 128:(f + 1) * 128], hmT2[slot][:, k, 0:n], start=(k == 0), stop=(k == 7))
                P.act(rl[:, 0:n], bk[:, 0:n], AF.Relu)
                P.tt(h2T[:, f, 0:n], rl[:, 0:n], rl[:, 0:n], ALU.mult)

        def mlp_down(n, slot, outname, orow0):
            ntile = (n + 127) // 128
            x2 = x2s_[slot]
            for t in range(ntile):
                tn = min(128, n - t * 128)
                for hh, bk in enumerate([B4, B5]):
                    for f in range(32):
                        P.matmul(bk[0:tn, :], h2T[:, f, t * 128:t * 128 + tn], w_dn[:, f, hh * 512:(hh + 1) * 512],
                                 start=(f == 0), stop=(f == 31))
                    P.tt(x2[t][0:tn, hh * 512:(hh + 1) * 512], bk[0:tn, :], x2[t][0:tn, hh * 512:(hh + 1) * 512], ALU.add)
                P.memset(ss3[0:tn, :], 0.0)
                P.act(sq3[0:tn, :], x2[t][0:tn, :], AF.Square, accum_out=ss3[0:tn, :])
                P.ts(ss3[0:tn, :], ss3[0:tn, :], 1.0 / 1024.0, ALU.mult, NORM_EPS, ALU.add)
                P.act(ss3[0:tn, :], ss3[0:tn, :], AF.Ln)
                P.act(rs3[0:tn, :], ss3[0:tn, :], AF.Exp, scale=-0.5)
                P.stt(yo[t][0:tn, :], x2[t][0:tn, :], rs3[0:tn, 0:1], gfin[0:tn, :], ALU.mult, ALU.mult)
                P.dma(D[outname][orow0 + t * 128:orow0 + t * 128 + tn, :], yo[t][0:tn, :], is_output=True)

        jobs = [(bi * NT, NT, "yp", bi * NT) for bi in range(NTOK // NT)] + [(NTOK, 32, "ys", 0)]
        mlp_prep(jobs[0][0], jobs[0][1], 0)
        for ji, (row0, n, oname, orow0) in enumerate(jobs):
            slot = ji % 2
            mlp_up(n, slot)
            if ji + 1 < len(jobs):
                mlp_prep(jobs[ji + 1][0], jobs[ji + 1][1], 1 - slot)
            mlp_down(n, slot, oname, orow0)
        P.finalize()
    return nc


NORM_EPS = 1e-6
GN_EPS = 64e-5


def _t5_bucket(rel):
    half, max_exact = 16, 8
    n = np.abs(rel)
    large = max_exact + (np.log(np.maximum(n, 1).astype(np.float32) / max_exact)
                         / np.float32(math.log(128 / max_exact)) * (half - max_exact)).astype(np.int32)
    large = np.minimum(large, half - 1)
    return np.where(rel > 0, half, 0) + np.where(n < max_exact, n, large)


_CACHE = {}


def make_in_maps(A):
    f32 = np.float32
    w_in = A["w_in"][0]
    qperm = np.concatenate([np.concatenate([np.arange(m * 64, m * 64 + 64), np.arange((4 + m) * 64, (4 + m) * 64 + 64)])
                            for m in range(4)])
    w_in_l = np.ascontiguousarray(np.concatenate([w_in[:, qperm], w_in[:, 512:]], axis=1))
    fm = lambda v, k: np.ascontiguousarray(v.reshape(k, 128).T)
    gvec = np.stack([fm(A["norm_mix_g"][0], 8), fm(A["norm_cross_g"][0], 8), fm(A["norm_mem_g"][0], 8),
                     fm(A["norm_mlp_g"][0], 8)], axis=1).astype(f32)
    pvec = np.stack([fm(A["rwkv_w0"][0], 4), fm(A["rwkv_a0"][0], 4), fm(A["rwkv_k_k"][0], 4), fm(A["rwkv_k_a"][0], 4),
                     fm(A["rwkv_ln_w"][0], 4), fm(A["rwkv_ln_b"][0], 4), fm(A["rwkv_r_k"][0].reshape(-1), 4)], axis=1).astype(f32)
    shared = {
        "w_in": w_in_l, "w_out": A["w_out"][0], "w_cq": A["w_cq"][0], "w_mk": A["w_mk"][0], "w_mv": A["w_mv"][0],
        "w_co": A["w_co"][0], "w_up": A["w_up"][0], "w_down": A["w_down"][0],
        "w2a2": np.ascontiguousarray(np.concatenate([A["rwkv_w2"][0], A["rwkv_a2"][0]], axis=0)),
        "g2": A["rwkv_g2"][0], "gvec": gvec, "gfin": A["norm_final_g"].reshape(1, 1024),
        "mu": fm(A["rwkv_mu"][0], 14), "pvec": pvec,
        "sink": A["attn_sink"][0].reshape(8, 1), "table": A["rel_bias_table"],
    }
    ident = np.eye(128, dtype=f32)
    bones = np.zeros((128, 128), f32); bones[:64, :64] = 1; bones[64:, 64:] = 1
    s_ = np.arange(64)[:, None]; t_ = np.arange(64)[None, :]
    mar = np.stack([(s_ < t_), (s_ <= t_)], axis=1).astype(f32)
    mts = (np.arange(64)[:, None] > np.arange(64)[None, :]).astype(f32)
    rmask = np.ones((128, NT), f32); rmask[:, ::64] = 0
    maskadd = np.zeros((128, 2, 128), f32)
    for kt in range(2):
        for k in range(128):
            kc = (kt * 128 - 128 + k) // 64
            for qc in range(2):
                if not (qc - 2 <= kc <= qc):
                    maskadd[k, kt, qc * 64:(qc + 1) * 64] = NEG
    rel = np.arange(384) - 255
    oh = np.zeros((32, 384), f32)
    oh[_t5_bucket(rel.astype(np.int32)), np.arange(384)] = 1
    oh[:, 383] = 0
    consts = {"c_ident": ident, "c_J": np.ascontiguousarray(ident[::-1]), "c_bones": bones, "c_bones64": bones / 64.0,
              "c_ident2": np.concatenate([np.eye(64, dtype=f32)] * 2, axis=0), "c_mar": mar, "c_mts": mts,
              "c_rmask": rmask, "c_maskadd": maskadd, "c_oh": oh}
    shared.update(consts)
    shared = {k: np.ascontiguousarray(v, dtype=f32) for k, v in shared.items()}
    in_maps = []
    for c in range(8):
        b, h = c // 2, c % 2
        m = dict(shared)
        m["xp"] = np.ascontiguousarray(A["x_prompt"][b, h * NTOK:(h + 1) * NTOK])
        m["xprev"] = np.ascontiguousarray(A["x_prompt"][b, 0:NTOK]) if h == 1 else np.zeros((NTOK, 1024), f32)
        m["xs"] = np.ascontiguousarray(A["x_sample"][c])
        m["mem"] = np.ascontiguousarray(A["mem_prompt"][b])
        m["ck"] = np.ascontiguousarray(A["cache_attn_k"][0, c].reshape(128, 128))
        m["cv"] = np.ascontiguousarray(A["cache_attn_v"][0, c].reshape(128, 128))
        m["cmk"] = np.ascontiguousarray(A["cache_mem_k"][0, c].reshape(256, 512))
        m["cmv"] = np.ascontiguousarray(A["cache_mem_v"][0, c].reshape(256, 512))
        m["sshT"] = fm(A["state_shift"][0, c, 0], 14).astype(f32)
        m["swkv"] = np.ascontiguousarray(A["state_wkv"][0, c])
        m["negmask"] = np.full((128, 1), NEG if h == 0 else 0.0, f32)
        in_maps.append(m)
    return in_maps


def assemble(R):
    f32 = np.float32
    y_prompt = np.stack([np.concatenate([R[2 * b]["yp"], R[2 * b + 1]["yp"]], axis=0) for b in range(4)])
    y_sample = np.stack([R[c]["ys"] for c in range(8)])
    p_k = np.stack([R[2 * b + 1]["pk"].reshape(128, 2, 64) for b in range(4)])[None]
    p_v = np.stack([R[2 * b + 1]["pv"].reshape(128, 2, 64) for b in range(4)])[None]
    p_mk = np.stack([R[2 * b]["pmk"].reshape(256, 4, 128) for b in range(4)])[None]
    p_mv = np.stack([R[2 * b]["pmv"].reshape(256, 4, 128) for b in range(4)])[None]
    p_sh = np.stack([R[2 * b + 1]["psh"].reshape(1, 1792) for b in range(4)])[None]
    p_S = np.stack([np.transpose(R[2 * b + 1]["pwkv"], (1, 0, 2)) for b in range(4)])[None]
    s_k = np.stack([R[c]["sk"].reshape(128, 2, 64) for c in range(8)])[None]
    s_v = np.stack([R[c]["sv"].reshape(128, 2, 64) for c in range(8)])[None]
    s_sh = np.stack([R[c]["ssh"].reshape(1, 1792) for c in range(8)])[None]
    s_S = np.stack([np.transpose(R[c]["swkvo"], (1, 0, 2)) for c in range(8)])[None]
    outs = (y_prompt, y_sample, p_k, p_v, p_mk, p_mv, p_sh, p_S, s_k, s_v, s_sh, s_S)
    return tuple(np.ascontiguousarray(o, dtype=f32) for o in outs)


def kernel(**inp):
    A = {k: np.asarray(v) for k, v in inp.items()}
    if "nc" not in _CACHE:
        nc = bass.Bass("TRN2", target_bir_lowering=False)
        build_program(nc)
        _CACHE["nc"] = nc
    nc = _CACHE["nc"]
    in_maps = make_in_maps(A)
    res = run_bass_kernel_spmd(nc, in_maps, core_ids=list(range(8)))
    return assemble(res.results)
```

```python
import numpy as np
import concourse.bass as bass
import concourse.mybir as mybir
from concourse.bass_utils import run_bass_kernel_spmd

F32 = mybir.dt.float32
BF16 = mybir.dt.bfloat16
F32R = mybir.dt.float32r
I32 = mybir.dt.int32
ALU = mybir.AluOpType
AF = mybir.ActivationFunctionType
AX = mybir.AxisListType

SAME_ENG_SYNC = True


class Buf:
    __slots__ = ("name", "last_w", "readers", "excl", "pe_last")

    def __init__(self, name):
        self.name = name
        self.excl = False
        self.pe_last = None
        self.last_w = None
        self.readers = []


class V:
    __slots__ = ("ap", "buf")

    def __init__(self, ap, buf):
        self.ap = ap
        self.buf = buf

    def __getitem__(self, k):
        return V(self.ap[k], self.buf)

    def bitcast(self, dt):
        return V(self.ap.bitcast(dt), self.buf)

    def rearrange(self, s, **kw):
        return V(self.ap.rearrange(s, **kw), self.buf)

    def bc(self, shape):
        return V(self.ap.to_broadcast(shape), self.buf)

    def unsq(self, ax):
        return V(self.ap.unsqueeze(ax), self.buf)


class Op:
    __slots__ = ("eng", "idx", "fn", "deps", "needed", "is_dma", "sem", "val", "prev_val", "cnt")

    def __init__(self, eng, idx, fn, is_dma):
        self.eng = eng
        self.idx = idx
        self.fn = fn
        self.deps = []
        self.needed = False
        self.is_dma = is_dma
        self.sem = None
        self.val = 0
        self.prev_val = 0
        self.cnt = 0


ENGS = ["pe", "act", "dve", "pool", "sp"]


class Prog:
    def __init__(self, nc, n_dma_sems=40):
        self.nc = nc
        self.ops = {e: [] for e in ENGS}
        self.known = {e: {} for e in ENGS}
        self.known_dma = {e: set() for e in ENGS}
        self.n_dma_sems = n_dma_sems
        self.dma_uses = [0] * n_dma_sems
        self.dma_rr = 0
        self.dma_rr_sw = 0
        self.n_hw = n_dma_sems - 12
        self.out_dmas = []
        self.bufs = []

    def buf(self, name):
        b = Buf(name)
        self.bufs.append(b)
        return b

    def _emit(self, eng, fn, outs, ins, is_dma=False, force=()):
        op = Op(eng, len(self.ops[eng]), fn, is_dma)
        deps = []
        for v in ins:
            b = v.buf
            if b.last_w is not None:
                deps.append(b.last_w)
            if b.excl:
                deps.extend(r for r in b.readers if r.eng != eng)
        for v in outs:
            b = v.buf
            if b.last_w is not None:
                deps.append(b.last_w)
            deps.extend(b.readers)
        deps.extend(force)
        best = {}
        for d in deps:
            if d is op:
                continue
            if d.is_dma:
                if id(d) in self.known_dma[eng]:
                    continue
                self.known_dma[eng].add(id(d))
                op.deps.append(d)
            else:
                if d.eng == eng and (eng == "pe" or not SAME_ENG_SYNC) and d not in force:
                    continue
                if d.idx <= self.known[eng].get(d.eng, -1):
                    continue
                if d.eng not in best or best[d.eng].idx < d.idx:
                    best[d.eng] = d
        for e2, d in best.items():
            self.known[eng][e2] = d.idx
            d.needed = True
            op.deps.append(d)
        if is_dma:
            if eng == "pool":
                s = self.n_hw + self.dma_rr_sw
                self.dma_rr_sw = (self.dma_rr_sw + 1) % (self.n_dma_sems - self.n_hw)
            else:
                s = self.dma_rr
                self.dma_rr = (self.dma_rr + 1) % self.n_hw
            op.sem = s
            op.prev_val = 16 * self.dma_uses[s]
            self.dma_uses[s] += 1
            op.val = 16 * self.dma_uses[s]
        self.ops[eng].append(op)
        seen = set()
        for v in outs:
            b = v.buf
            if id(b) in seen:
                continue
            seen.add(id(b))
            b.last_w = op
            b.readers = []
        for v in ins:
            b = v.buf
            if id(b) in seen:
                continue
            b.readers.append(op)
        return op

    def dma(self, out, in_, eng="sp", is_output=False, **kw):
        op = self._emit(eng, lambda e: e.dma_start(out=out.ap, in_=in_.ap, **kw), [out], [in_], is_dma=True)
        if is_output:
            self.out_dmas.append(op)
        return op

    def _rg_force(self, out, src):
        bp = src.ap.base_partition
        rg = bp() if callable(bp) else bp
        prev = out.buf.pe_last
        force = ()
        if prev is not None and prev[0] != rg:
            force = (prev[1],)
        return rg, force

    def matmul(self, out, lhsT, rhs, start=True, stop=True, **kw):
        rg, force = self._rg_force(out, lhsT)
        op = self._emit("pe", lambda e: e.matmul(out.ap, lhsT.ap, rhs.ap, start=start, stop=stop, **kw),
                        [out], [lhsT, rhs] + ([] if start else [out]), force=force)
        out.buf.pe_last = (rg, op)
        return op

    def transpose(self, out, in_, ident):
        rg, force = self._rg_force(out, in_)
        op = self._emit("pe", lambda e: e.transpose(out.ap, in_.ap, ident.ap), [out], [in_, ident], force=force)
        out.buf.pe_last = (rg, op)
        return op

    def act(self, out, in_, func, bias=None, scale=None, accum_out=None, eng="act"):
        kw = {}
        ins = [in_]
        outs = [out]
        if bias is not None:
            if isinstance(bias, V):
                kw["bias"] = bias.ap
                ins.append(bias)
            else:
                kw["bias"] = bias
        if scale is not None:
            if isinstance(scale, V):
                kw["scale"] = scale.ap
                ins.append(scale)
            else:
                kw["scale"] = scale
        if accum_out is not None:
            kw["accum_out"] = accum_out.ap
            outs.append(accum_out)
        return self._emit(eng, lambda e: e.activation(out.ap, in_.ap, func, **kw), outs, ins)

    def tt(self, out, in0, in1, op, eng="dve"):
        return self._emit(eng, lambda e: e.tensor_tensor(out.ap, in0.ap, in1.ap, op), [out], [in0, in1])

    def ts(self, out, in0, s1, op0, s2=None, op1=None, accum_out=None, eng="dve"):
        ins = [in0]
        outs = [out]
        a1 = s1.ap if isinstance(s1, V) else s1
        a2 = s2.ap if isinstance(s2, V) else s2
        if isinstance(s1, V):
            ins.append(s1)
        if isinstance(s2, V):
            ins.append(s2)
        kw = {}
        if op1 is not None:
            kw["op1"] = op1
        if accum_out is not None:
            kw["accum_out"] = accum_out.ap
            outs.append(accum_out)
        return self._emit(eng, lambda e: e.tensor_scalar(out.ap, in0.ap, a1, a2, op0, **kw), outs, ins)

    def stt(self, out, in0, scalar, in1, op0, op1, eng="dve"):
        ins = [in0, in1]
        a = scalar.ap if isinstance(scalar, V) else scalar
        if isinstance(scalar, V):
            ins.append(scalar)
        return self._emit(eng, lambda e: e.scalar_tensor_tensor(out.ap, in0.ap, a, in1.ap, op0, op1), [out], ins)

    def copy(self, out, in_, eng="dve"):
        if eng == "act":
            return self._emit("act", lambda e: e.copy(out.ap, in_.ap), [out], [in_])
        return self._emit(eng, lambda e: e.tensor_copy(out.ap, in_.ap), [out], [in_])

    def memset(self, out, val, eng="pool"):
        return self._emit(eng, lambda e: e.memset(out.ap, val), [out], [])

    def reduce(self, out, in_, op, axis=AX.X, eng="dve"):
        return self._emit(eng, lambda e: e.tensor_reduce(out.ap, in_.ap, axis, op), [out], [in_])

    def scan(self, out, d0, d1, initial, op0, op1):
        return self._emit("dve", lambda e: e.tensor_tensor_scan(out.ap, d0.ap, d1.ap, initial, op0, op1),
                          [out], [d0, d1])

    def generic(self, eng, fn, outs, ins):
        return self._emit(eng, fn, outs, ins)

    def barrier(self):
        lasts = {}
        for e in ENGS:
            for op in reversed(self.ops[e]):
                if not op.is_dma and op.fn is not None:
                    lasts[e] = op
                    break
        dma_waits = [(s_, 16 * u) for s_, u in enumerate(self.dma_uses) if u > 0]
        for e in ENGS:
            op = Op(e, len(self.ops[e]), None, False)
            for e2, d in lasts.items():
                if e2 == e:
                    continue
                if d.idx <= self.known[e].get(e2, -1):
                    continue
                self.known[e][e2] = d.idx
                d.needed = True
                op.deps.append(d)
            op.val = dma_waits
            self.ops[e].append(op)

    def finalize(self):
        nc = self.nc
        final_deps = [d for d in self.out_dmas]
        import contextlib
        with contextlib.ExitStack() as st:
            esems = {e: st.enter_context(nc.semaphore("s_" + e)) for e in ENGS}
            dsems = [st.enter_context(nc.semaphore("d%d" % i)) for i in range(self.n_dma_sems)]
            for e in ENGS:
                c = 0
                for op in self.ops[e]:
                    if op.is_dma:
                        continue
                    if op.needed:
                        c += 1
                    op.cnt = c
            block = st.enter_context(nc.Block())

            def run(e, eng_obj):
                for op in self.ops[e]:
                    for d in op.deps:
                        if d.is_dma:
                            eng_obj.wait_ge(dsems[d.sem], d.val)
                        else:
                            eng_obj.wait_ge(esems[d.eng], d.cnt)
                    if op.fn is None:
                        for s_, v_ in op.val:
                            eng_obj.wait_ge(dsems[s_], v_)
                        continue
                    if op.is_dma and op.prev_val > 0:
                        eng_obj.wait_ge(dsems[op.sem], op.prev_val)
                    ins = op.fn(eng_obj)
                    if op.is_dma:
                        ins.then_inc(dsems[op.sem], 16)
                    elif op.needed:
                        ins.then_inc(esems[e], 1)
                if e == "sp":
                    for d in final_deps:
                        eng_obj.wait_ge(dsems[d.sem], d.val)

            @block.sync
            def _(eng):
                run("sp", eng)

            @block.tensor
            def _(eng):
                run("pe", eng)

            @block.scalar
            def _(eng):
                run("act", eng)

            @block.vector
            def _(eng):
                run("dve", eng)

            @block.gpsimd
            def _(eng):
                run("pool", eng)


import contextlib
import math

NTOK = 2048
NT = 256
C0 = math.exp(-0.5)
NEG = -30000.0


class _Stop(Exception):
    pass


STOP = 99
SEQ_HALVES = False
NO_FILL = False


def build_program(nc):
    try:
        _build_program(nc)
    except _Stop:
        pass
    return nc


def _build_program(nc):
    P = Prog(nc)

    def chk(stage):
        if STOP == stage:
            P.finalize()
            raise _Stop()

    st = contextlib.ExitStack()
    D = {}

    def din(name, shape, dt=F32):
        D[name] = V(nc.dram_tensor(name, list(shape), dt, kind="ExternalInput").ap(), P.buf(name))

    def dout(name, shape, dt=F32):
        D[name] = V(nc.dram_tensor(name, list(shape), dt, kind="ExternalOutput").ap(), P.buf(name))

    def dscr(name, shape, dt=F32):
        D[name] = V(nc.dram_tensor(name, list(shape), dt, kind="Internal").ap(), P.buf(name))

    def sb(name, shape, dt=F32, stack=None):
        t = (stack or st).enter_context(nc.sbuf_tensor("s_" + name, list(shape), dt))
        return V(t[tuple(slice(None) for _ in shape)], P.buf(name))

    def ps(name, shape, dt=F32):
        t = st.enter_context(nc.psum_tensor("p_" + name, list(shape), dt))
        b = P.buf(name)
        b.excl = True
        return V(t[tuple(slice(None) for _ in shape)], b)

    for n, s in [("xp", (NTOK, 1024)), ("xprev", (NTOK, 1024)), ("xs", (32, 1024)), ("mem", (256, 1024)),
                 ("ck", (128, 128)), ("cv", (128, 128)), ("cmk", (256, 512)), ("cmv", (256, 512)),
                 ("sshT", (128, 14)), ("swkv", (8, 64, 64)),
                 ("w_in", (1024, 2560)), ("w_out", (1024, 1024)), ("w_cq", (1024, 512)), ("w_mk", (1024, 512)),
                 ("w_mv", (1024, 512)), ("w_co", (512, 1024)), ("w_up", (1024, 4096)), ("w_down", (4096, 1024)),
                 ("w2a2", (128, 512)), ("g2", (128, 512)),
                 ("gvec", (128, 4, 8)), ("gfin", (1, 1024)), ("mu", (128, 14)), ("pvec", (128, 7, 4)),
                 ("sink", (8, 1)), ("table", (32, 8)),
                 ("c_ident", (128, 128)), ("c_J", (128, 128)), ("c_bones", (128, 128)), ("c_bones64", (128, 128)),
                 ("c_ident2", (128, 64)), ("c_mar", (64, 2, 64)), ("c_mts", (64, 64)), ("c_rmask", (128, NT)),
                 ("c_maskadd", (128, 2, 128)), ("c_oh", (32, 384)), ("negmask", (128, 1))]:
        din(n, s)
    for n, s in [("yp", (NTOK, 1024)), ("ys", (32, 1024)), ("pk", (128, 128)), ("pv", (128, 128)),
                 ("pmk", (256, 512)), ("pmv", (256, 512)), ("psh", (14, 128)), ("pwkv", (64, 8, 64)),
                 ("sk", (128, 128)), ("sv", (128, 128)), ("ssh", (14, 128)), ("swkvo", (64, 8, 64))]:
        dout(n, s)
    dscr("x2s", (NTOK + 32, 1024))
    dscr("tbd", (8, 384))
    dscr("wupb", (1024, 4096), BF16)
    dscr("wdnb", (4096, 1024), BF16)

    with st:
        ph1 = contextlib.ExitStack()
        st.enter_context(ph1)
        GA0 = ps("GA0", [128, 512]); GA1 = ps("GA1", [128, 512]); GB = ps("GB", [128, 512])
        GT = ps("GT", [128, 1024], BF16)
        B4 = ps("B4", [128, 512]); B5 = ps("B5", [128, 512]); B6 = ps("B6", [128, 512]); B7 = ps("B7", [128, 512])
        gemm_banks = [GA0, GA1, GB]
        gemm_rr = [0]

        def gbank():
            b = gemm_banks[gemm_rr[0] % 3]
            gemm_rr[0] += 1
            return b

        ident = sb("ident", [128, 128]); P.dma(ident, D["c_ident"])
        Jm = sb("Jm", [128, 128]); P.dma(Jm, D["c_J"])
        bones = sb("bones", [128, 128]); P.dma(bones, D["c_bones"])
        bones64 = sb("bones64", [128, 128]); P.dma(bones64, D["c_bones64"])
        ident2 = sb("ident2", [128, 64]); P.dma(ident2, D["c_ident2"])
        mar = sb("mar", [64, 2, 64]); P.dma(mar, D["c_mar"])
        mts = sb("mts", [64, 64]); P.dma(mts, D["c_mts"])
        rmask = sb("rmask", [128, NT]); P.dma(rmask, D["c_rmask"])
        maskadd = sb("maskadd", [128, 2, 128]); P.dma(maskadd, D["c_maskadd"])
        negmask = sb("negmask", [128, 1]); P.dma(negmask, D["negmask"])
        identb = sb("identb", [128, 128], BF16); P.copy(identb, ident)
        identr = sb("identr", [128, 128], F32R); P.copy(identr, ident)
        onesb = sb("onesb", [128, 128], BF16); P.memset(onesb, 1.0)
        gvec = sb("gvec", [128, 4, 8]); P.dma(gvec, D["gvec"])
        mu = sb("mu", [128, 14]); P.dma(mu, D["mu"])
        pvec = sb("pvec", [128, 7, 4]); P.dma(pvec, D["pvec"])
        npv = sb("npv", [128, 2, 4]); P.ts(npv, pvec[:, 0:2, :], -1.0, ALU.mult)
        sink = sb("sink", [8, 1]); P.dma(sink, D["sink"])
        eps_t = sb("eps_t", [128, 1]); P.memset(eps_t, 0.0)

        w_in = sb("w_in", [128, 8, 2560], BF16, ph1)
        w2a2 = sb("w2a2", [128, 512], BF16, ph1)
        g2 = sb("g2", [128, 512], BF16, ph1)
        w_out = sb("w_out", [128, 8, 1024], BF16, ph1)
        w_cq = sb("w_cq", [128, 8, 512], BF16, ph1)
        w_co = sb("w_co", [128, 4, 1024], BF16, ph1)
        xt = [sb("xt%d" % i, [128, 1024], F32, ph1) for i in range(2)]
        xnT = sb("xnT", [128, 8, NT], BF16, ph1)
        xsc = sb("xsc", [128, 1024], BF16, ph1)
        sq_junk = xsc
        ss = sb("ss", [128, 1], F32, ph1); rstd = sb("rstd", [128, 1], F32, ph1)
        hcT = xnT
        mkT = sb("mkT", [128, 4, 256], BF16, ph1)
        mvt = sb("mvt", [128, 2, 512], BF16, ph1)
        s_sb = sb("s_sb", [128, 2, 4, 128], F32, ph1)
        mtmp = s_sb.rearrange("p a m q -> p (a m q)")[:, 0:512]
        stg = s_sb.rearrange("p a m q -> p (a m q)")[:, 512:1024]
        biasT = sb("biasT", [128, 2, 8, 128], F32, ph1)
        mks = contextlib.ExitStack()
        w_mk = sb("w_mk", [128, 8, 512], BF16, mks); P.dma(w_mk, D["w_mk"].rearrange("(k p) n -> p k n", p=128), eng="pool")
        w_mv = sb("w_mv", [128, 8, 512], BF16, mks); P.dma(w_mv, D["w_mv"].rearrange("(k p) n -> p k n", p=128), eng="pool")
        src = D["w_in"].rearrange("(k p) n -> p k n", p=128)
        for k in range(8):
            P.dma(w_in[:, k, :], src[:, k, :], eng="pool")
        P.dma(w2a2, D["w2a2"], eng="pool")
        P.dma(g2, D["g2"], eng="pool")
        for g in range(2):
            P.dma(w_out[g * 64:(g + 1) * 64, 0:4, :],
                  D["w_out"][g * 256:(g + 1) * 256, :].rearrange("(m d) n -> d m n", d=64), eng="pool")
        P.dma(w_out[:, 4:8, :], D["w_out"][512:1024, :].rearrange("(k p) n -> p k n", p=128), eng="pool")
        P.dma(w_cq, D["w_cq"].rearrange("(k p) n -> p k n", p=128), eng="pool")
        P.dma(w_co, D["w_co"].rearrange("(k p) n -> p k n", p=128), eng="pool")
        bg = []
        for k in range(8):
            bg.append((D["wupb"][k * 128:(k + 1) * 128, :], D["w_up"][k * 128:(k + 1) * 128, :]))
        for k in range(8):
            bg.append((D["wdnb"][k * 512:(k + 1) * 512, :], D["w_down"][k * 512:(k + 1) * 512, :]))

        chk(0)
        with mks as tmp:
            table = sb("table", [32, 8], F32, tmp); P.dma(table, D["table"])
            oh = sb("oh", [32, 384], F32, tmp); P.dma(oh, D["c_oh"])
            tb = sb("tb", [8, 384], F32, tmp)
            P.matmul(GB[0:8, 0:384], table, oh)
            P.ts(tb, GB[0:8, 0:384], sink[0:8, 0:1], ALU.subtract)
            P.dma(D["tbd"], tb)
            XT = sb("XT", [128, 8, 128], F32, tmp)
            for kt in range(2):
                hank = bass.AP(tensor=D["tbd"].ap.tensor, offset=kt * 128, ap=[[1, 128], [384, 8], [1, 128]])
                P.dma(XT, V(hank, D["tbd"].buf))
                for hh in range(8):
                    bk = [GA0, GA1][hh % 2]
                    P.matmul(bk[:, 0:128], XT[:, hh, :], Jm)
                    P.tt(biasT[:, kt, hh, :], bk[:, 0:128], maskadd[:, kt, :], ALU.add)

            def rmsnorm_T(xtile, n, gidx, outT, c0):
                P.memset(ss[0:n, :], 0.0)
                P.act(sq_junk[0:n, :], xtile[0:n, :], AF.Square, accum_out=ss[0:n, :])
                P.ts(ss[0:n, :], ss[0:n, :], 1.0 / 1024.0, ALU.mult, NORM_EPS, ALU.add)
                P.act(ss[0:n, :], ss[0:n, :], AF.Ln)
                P.act(rstd[0:n, :], ss[0:n, :], AF.Exp, scale=-0.5)
                P.act(xsc[0:n, :], xtile[0:n, :], AF.Copy, scale=rstd[0:n, :])
                for k in range(8):
                    P.transpose(GT[:, k * 128:k * 128 + n], xsc[0:n, k * 128:(k + 1) * 128], identb[0:n, 0:n])
                P.tt(outT[:, :, c0:c0 + n], GT.rearrange("p (k t) -> p k t", k=8)[:, :, 0:n],
                     gvec[:, gidx, :].unsq(2).bc([128, 8, n]), ALU.mult)

            def gemm_fm(lhs_w, col0, rhsT, n, evac):
                bk = gbank()
                for k in range(8):
                    P.matmul(bk[:, 0:n], lhs_w[:, k, col0:col0 + 128], rhsT[:, k, 0:n], start=(k == 0), stop=(k == 7))
                evac(bk[:, 0:n])

            chk(1)
            mnT = hcT
            for t in range(2):
                P.dma(xt[t], D["mem"][t * 128:(t + 1) * 128, :])
                rmsnorm_T(xt[t], 128, 2, mnT, t * 128)
            for hd in range(4):
                gemm_fm(w_mk, hd * 128, mnT, 256, lambda src, hd=hd: P.copy(mkT[:, hd, :], src, eng="act"))
            for t in range(2):
                for wi, (wm, oname) in enumerate([(w_mk, "pmk"), (w_mv, "pmv")]):
                    bk = gbank()
                    for k in range(8):
                        P.matmul(bk[:, :], mnT[:, k, t * 128:(t + 1) * 128], wm[:, k, :], start=(k == 0), stop=(k == 7))
                    P.copy(mtmp, bk, eng="act")
                    P.dma(D[oname][t * 128:(t + 1) * 128, :], mtmp, is_output=True)
                    if wi == 1:
                        P.copy(mvt[:, t, :], bk)

            P.barrier()
        qT = sb("qT", [128, 4, NT], BF16, ph1)
        kTh = sb("kTh", [128, 128 + NT], BF16, ph1)
        vth = sb("vth", [128, 3, 128], BF16, ph1)
        kvtok = sb("kvtok", [128, 256], F32, ph1)
        zr = sb("zr", [128, 14, NT + 1], F32, ph1)
        zhist = sb("zhist", [128, 14], F32, ph1)
        dtmp = sb("dtmp", [128, NT], F32, ph1)
        twd = sb("twd", [128, NT], BF16, ph1)
        sgd = sb("sgd", [128, NT], BF16, ph1)
        mixT = sb("mixT", [128, 8, NT], BF16, ph1)
        H = [[sb("H%d_%d" % (ft, i), [128, 64], F32R, ph1) for i in range(2)] for ft in range(4)]
        hpar = [0, 0, 0, 0]

        sw = sb("sw", [128, NT], F32, ph1); cs = sb("cs", [128, NT], F32, ph1)
        Pm2 = [sb("Pm%d" % i, [128, NT], F32, ph1) for i in range(2)]; Pinv = sb("Pinv", [128, NT], F32, ph1)
        Pprev = dtmp; Ee = sb("Ee", [128, NT], F32, ph1)
        asig = sb("asig", [128, NT], F32, ph1)
        kkr = sb("kkr", [128, NT], F32, ph1)
        rn = sb("rn", [128, NT], F32, ph1); kk = rn; kmod = sb("kmod", [128, NT], F32, ph1)
        bb = sb("bb", [128, NT], F32, ph1); t1 = sb("t1", [128, NT], F32, ph1)
        BK2 = [sb("BK%d" % i, [128, 2, NT], F32R, ph1) for i in range(2)]
        AR2 = [sb("AR%d" % i, [128, NT // 32, 2, 32], F32R, ph1) for i in range(2)]
        at2 = [sb("at%d" % i, [128, NT], F32R, ph1) for i in range(2)]
        bh2 = [sb("bh%d" % i, [128, NT], F32R, ph1) for i in range(2)]
        kh2 = [sb("kh%d" % i, [128, NT], F32R, ph1) for i in range(2)]
        bonus2 = [sb("bonus%d" % i, [128, NT], F32, ph1) for i in range(2)]
        yT = sb("yT", [128, NT], F32, ph1)
        vv2 = [sb("vv%d" % i, [128, NT], F32R, ph1) for i in range(2)]
        cen = kkr
        tokm_h = [sb("tokm%d" % i, [64, 2, 3, 128], F32R, ph1) for i in range(2)]
        X0_h = [sb("X0_%d" % i, [64, 2, 2, 128], F32R, ph1) for i in range(2)]
        X1_h = [sb("X1_%d" % i, [64, 2, 2, 128], F32R, ph1) for i in range(2)]
        SM_h = [sb("SM%d" % i, [64, 2, 2, 2, 128], F32R, ph1) for i in range(2)]
        Am_h = [sb("Am%d" % i, [64, 2, 2, 64], F32R, ph1) for i in range(2)]
        ATT_h = [sb("ATT%d" % i, [64, 2, 2, 2, 64], F32R, ph1) for i in range(2)]
        Gbd_h = [sb("Gbd%d" % i, [128, 2, 128], F32R, ph1) for i in range(2)]
        Cst_h = [sb("Cst%d" % i, [128, 2, 64], F32, ph1) for i in range(2)]
        RpT_h = [sb("RpT%d" % i, [128, 2, 64], F32R, ph1) for i in range(2)]
        pT = sb("pT", [128, 2, 4, 128], BF16, ph1)
        rden = sb("rden", [128, 512], F32, ph1)
        x1 = xt
        qcT = qT
        pcT = sb("pcT", [128, 2, NT], BF16, ph1)
        ocT = sb("ocT", [128, 4, NT], BF16, ph1)

        def block(xsrc, n, L, mode, first_pair_mask=False, row0=0, last=False):
            nch = n // L
            nlev = int(math.log2(L))
            ntile = (n + 127) // 128
            for t in range(ntile):
                tn = min(128, n - t * 128)
                P.dma(xt[t][0:tn, :], xsrc[t * 128:t * 128 + tn, :])
                rmsnorm_T(xt[t], tn, 0, xnT, t * 128)
            if mode != "prefix":
                for m in range(4):
                    gemm_fm(w_in, m * 128, xnT, n, lambda src, m=m: P.copy(qT[:, m, 0:n], src, eng="act"))
            gemm_fm(w_in, 512, xnT, n, lambda src: P.copy(kTh[:, 128:128 + n], src, eng="act"))
            for m in range(14):
                gemm_fm(w_in, 768 + m * 128, xnT, n,
                        lambda src, m=m: P.copy(zr[:, m, 1:n + 1], src, eng=("act" if m % 2 else "dve")))
            chk(30)
            for t in range(ntile):
                tn = min(128, n - t * 128)
                bk = gbank()
                for k in range(8):
                    P.matmul(bk[0:tn, 0:256], xnT[:, k, t * 128:t * 128 + tn], w_in[:, k, 512:768],
                             start=(k == 0), stop=(k == 7))
                P.copy(vth[0:tn, 1 + t, :], bk[0:tn, 128:256], eng="act")
                if last and t == ntile - 1:
                    P.copy(kvtok[0:tn, :], bk[0:tn, 0:256])
            chk(31)
            if bg and mode != "sample":
                o_, i_ = bg.pop(0)
                P.dma(o_, i_, eng="pool")
            P.copy(zhist, zr[:, :, n])
            def tshift(m):
                P.tt(dtmp[:, 0:n], zr[:, m, 0:n], zr[:, m, 1:n + 1], ALU.subtract, eng="pool")
                P.stt(zr[:, m, 1:n + 1], dtmp[:, 0:n], mu[:, m:m + 1], zr[:, m, 1:n + 1], ALU.mult, ALU.add)

            tshift(12)
            tshift(13)
            zs = lambda m: zr[:, m, 1:n + 1]
            chk(32)
            P.act(twd[0:64, 0:n], zr[0:64, 12, 1:n + 1], AF.Tanh)
            P.copy(twd[64:128, 0:n], zr[64:128, 12, 1:n + 1], eng="act")
            P.act(t1[:, 0:n], zs(13), AF.Exp, scale=-1.0)
            P.act(t1[:, 0:n], t1[:, 0:n], AF.Ln, bias=1.0)
            P.act(sgd[:, 0:n], t1[:, 0:n], AF.Exp, scale=-1.0)
            chk(33)
            def _bufs(ft):
                par = ft % 2
                ARv_ = AR2[par].rearrange("p c a t -> p (c a t)")[:, 0:2 * n].rearrange("p (c a t) -> p c a t", a=2, t=L)
                return at2[par], bh2[par], kh2[par], vv2[par], BK2[par], Pm2[par], bonus2[par], ARv_

            def pull(g, k, force=False):
                if g is None or (NO_FILL and not force):
                    return
                for _ in range(k):
                    try:
                        next(g)
                    except StopIteration:
                        return

            def E(ft):
                at_, bh_, kh_, vv_, BK_, Pm_, bonus_, ARv = _bufs(ft)
                GTf = GT.bitcast(F32)
                ebanks = [GTf, GTf]

                def gbank():
                    b = ebanks[ecnt[0] % 2]
                    ecnt[0] += 1
                    return b
                for m_ in (4 + ft, ft, 8 + ft):
                    tshift(m_)
                    yield
                r_, k_, v_ = zs(ft), zs(4 + ft), zs(8 + ft)
                pp = lambda i: pvec[:, i, ft:ft + 1]
                bk = gbank()
                P.matmul(bk[:, 0:n], w2a2[0:64, ft * 128:(ft + 1) * 128], twd[0:64, 0:n])
                P.act(sw[:, 0:n], bk[:, 0:n], AF.Exp, scale=-1.0, bias=npv[:, 0, ft:ft + 1])
                P.act(sw[:, 0:n], sw[:, 0:n], AF.Ln, bias=1.0)
                P.act(sw[:, 0:n], sw[:, 0:n], AF.Exp, scale=-1.0)
                yield
                bk = gbank()
                P.matmul(bk[:, 0:n], w2a2[64:128, ft * 128:(ft + 1) * 128], twd[64:128, 0:n])
                P.act(asig[:, 0:n], bk[:, 0:n], AF.Exp, scale=-1.0, bias=npv[:, 1, ft:ft + 1])
                P.act(asig[:, 0:n], asig[:, 0:n], AF.Ln, bias=1.0)
                P.act(asig[:, 0:n], asig[:, 0:n], AF.Exp, scale=-1.0)
                yield
                P.scan(cs[:, 0:n], rmask[:, 0:n], sw[:, 0:n], 0.0, ALU.mult, ALU.add)
                yield
                P.act(Pm_[:, 0:n], cs[:, 0:n], AF.Exp, scale=-C0)
                yield
                P.act(Pinv[:, 0:n], cs[:, 0:n], AF.Exp, scale=C0)
                yield
                P.tt(dtmp[:, 0:n], cs[:, 0:n], sw[:, 0:n], ALU.subtract, eng="pool")
                yield
                P.act(Pprev[:, 0:n], dtmp[:, 0:n], AF.Exp, scale=-C0)
                yield
                csv = cs[:, 0:n].rearrange("p (c t) -> p c t", t=L)
                P.tt(t1[:, 0:n].rearrange("p (c t) -> p c t", t=L), csv[:, :, L - 1:L].bc([128, nch, L]), csv, ALU.subtract)
                yield
                P.act(Ee[:, 0:n], t1[:, 0:n], AF.Exp, scale=-C0)
                yield
                chk(34)
                P.ts(kkr[:, 0:n], k_, pp(2), ALU.mult)
                yield
                P.tt(t1[:, 0:n], kkr[:, 0:n], kkr[:, 0:n], ALU.mult, eng="pool")
                yield
                bk = gbank()
                P.matmul(bk[:, 0:n], bones, t1[:, 0:n])
                P.ts(rn[:, 0:n], bk[:, 0:n], 1e-24, ALU.max)
                yield
                P.act(rn[:, 0:n], rn[:, 0:n], AF.Ln)
                yield
                P.act(rn[:, 0:n], rn[:, 0:n], AF.Exp, scale=-0.5)
                yield
                P.tt(kk[:, 0:n], kkr[:, 0:n], rn[:, 0:n], ALU.mult)
                yield
                P.ts(t1[:, 0:n], asig[:, 0:n], -1.0, ALU.add, pp(3), ALU.mult)
                yield
                P.stt(kmod[:, 0:n], t1[:, 0:n], 1.0, k_, ALU.add, ALU.mult)
                yield
                P.tt(bb[:, 0:n], kk[:, 0:n], asig[:, 0:n], ALU.mult, eng="pool")
                yield
                chk(35)
                nv = lambda a: a[:, 0:n].rearrange("p (c t) -> p c t", t=L)
                P.stt(at_[:, 0:n], kk[:, 0:n], -1.0, Pprev[:, 0:n], ALU.mult, ALU.mult)
                yield
                P.copy(ARv[:, :, 0, :], nv(at_), eng="act")
                yield
                P.tt(ARv[:, :, 1, :], nv(r_) if False else r_.rearrange("p (c t) -> p c t", t=L), nv(Pm_), ALU.mult)
                yield
                P.tt(BK_[:, 0, 0:n], bb[:, 0:n], Pinv[:, 0:n], ALU.mult)
                yield
                P.tt(BK_[:, 1, 0:n], kmod[:, 0:n], Pinv[:, 0:n], ALU.mult, eng="pool")
                yield
                P.tt(bh_[:, 0:n], bb[:, 0:n], Ee[:, 0:n], ALU.mult)
                yield
                P.tt(kh_[:, 0:n], kmod[:, 0:n], Ee[:, 0:n], ALU.mult, eng="pool")
                yield
                if mode != "prefix":
                    P.stt(t1[:, 0:n], r_, pp(6), kmod[:, 0:n], ALU.mult, ALU.mult)
                    bk = gbank()
                    P.matmul(bk[:, 0:n], bones, t1[:, 0:n])
                    P.tt(bonus_[:, 0:n], bk[:, 0:n], v_, ALU.mult)
                chk(36)
                P.copy(vv_[:, 0:n], v_, eng="act")
                yield

            def CG(ft, filler, gnf):
                at_, bh_, kh_, vv_, BK_, Pm_, bonus_, ARv = _bufs(ft)
                r_, k_, v_ = zs(ft), zs(4 + ft), zs(8 + ft)
                pp = lambda i: pvec[:, i, ft:ft + 1]
                Lc = slice(0, L)
                nhv = (nch + 1) // 2
                HB = [[B4, B5, B6], [B7, GA0, GA1]]

                def half(hv):
                    nc2 = min(2, nch - hv * 2)
                    h0, h1, h2 = HB[hv]
                    tk, x0, x1, sm = tokm_h[hv], X0_h[hv], X1_h[hv], SM_h[hv]
                    Amv, ATv, gbd, cst, rp = Am_h[hv], ATT_h[hv], Gbd_h[hv], Cst_h[hv], RpT_h[hv]
                    for ci in range(nc2):
                        c = hv * 2 + ci
                        cc = slice(c * L, (c + 1) * L)
                        tb_ = [h0, h1][ci]
                        for i, srcv in enumerate([at_[:, cc], bh_[:, cc], kh_[:, cc], vv_[:, cc]]):
                            P.transpose(tb_.bitcast(F32R)[Lc, i * 128:(i + 1) * 128], srcv, identr)
                    yield
                    for ci in range(nc2):
                        tb_ = [h0, h1][ci]
                        P.copy(x0[Lc, ci, :, 0:64], tb_[Lc, 0:128].rearrange("p (h j) -> p h j", h=2), eng="act")
                        P.copy(tk[Lc, ci, :, :], tb_[Lc, 128:512].rearrange("p (a j) -> p a j", a=3))
                    yield
                    scb = [h2, h0]
                    for hf in range(2):
                        pr = slice(hf * 64, (hf + 1) * 64)
                        for ci in range(nc2):
                            c = hv * 2 + ci
                            cc = slice(c * L, (c + 1) * L)
                            bkv = scb[hf][Lc, :].rearrange("p (c b x) -> p c b x", c=2, b=2)
                            for b_ in range(2):
                                P.matmul(bkv[:, ci, b_, 0:2 * L], BK_[pr, b_, cc], ARv[pr, c, :, :])
                    for hf in range(2):
                        pr = slice(hf * 64, (hf + 1) * 64)
                        for ci in range(nc2):
                            c = hv * 2 + ci
                            cc = slice(c * L, (c + 1) * L)
                            P.matmul(h1[Lc, (ci * 2 + hf) * 64:(ci * 2 + hf) * 64 + L], ARv[pr, c, 0, :], BK_[pr, 0, cc])
                    yield
                    for hf in range(2):
                        src_ = scb[hf][Lc, :].rearrange("p (c b x) -> p c b x", c=2, b=2)[:, 0:nc2, :, 0:2 * L]
                        for ci in range(nc2):
                            P.tt(sm[Lc, ci, hf, :, 0:2 * L].rearrange("p b (a t) -> p b a t", a=2),
                                 src_[:, ci].rearrange("p b (a t) -> p b a t", a=2),
                                 mar[Lc, :, Lc].unsq(1).bc([L, 2, 2, L]), ALU.mult)
                    NNv = h1[Lc, 0:256].rearrange("p (c h s) -> p c h s", c=2, h=2)[:, 0:nc2, :, Lc]
                    P.tt(Amv[Lc, 0:nc2, :, Lc], NNv, mts[Lc, Lc].unsq(1).unsq(1).bc([L, nc2, 2, L]), ALU.mult)
                    P.copy(ATv[Lc, 0:nc2, :, 0, Lc], sm[Lc, 0:nc2, :, 0, Lc], eng="act")
                    P.tt(ATv[Lc, 0:nc2, :, 1, Lc], sm[Lc, 0:nc2, :, 0, Lc],
                         ident[Lc, Lc].unsq(1).unsq(1).bc([L, nc2, 2, L]), ALU.add)
                    yield
                    for ci in range(nc2):
                        for hf in range(2):
                            P.matmul(h2[Lc, (ci * 2 + hf) * 64:(ci * 2 + hf) * 64 + 64], sm[Lc, ci, hf, 1, Lc],
                                     tk[Lc, ci, 2, hf * 64:(hf + 1) * 64])
                    yield
                    P.copy(x0[Lc, 0:nc2, :, 64:128], h2[Lc, 0:256].rearrange("p (c h x) -> p c h x", c=2, h=2)[:, 0:nc2], eng="act")
                    yield
                    RA = h1[Lc, 0:256].rearrange("p (c h s) -> p c h s", c=2, h=2)
                    RB = h0[Lc, :].rearrange("p (c h a s) -> p c h a s", c=2, h=2, a=2)
                    for r in range(nlev):
                        needA = (r + 1 <= nlev - 1)
                        needAt = (r + 1 <= nlev - 2)
                        needT = (r >= 1)
                        for ci in range(nc2):
                            for hf in range(2):
                                if needA:
                                    P.matmul(RA[:, ci, hf, Lc], ATv[Lc, ci, hf, 0, Lc], Amv[Lc, ci, hf, Lc])
                                if needAt and needT and L == 64:
                                    P.matmul(RB[:, ci, hf, :, Lc], Amv[Lc, ci, hf, Lc], ATv[Lc, ci, hf, :, Lc])
                                else:
                                    if needAt:
                                        P.matmul(RB[:, ci, hf, 0, Lc], Amv[Lc, ci, hf, Lc], ATv[Lc, ci, hf, 0, Lc])
                                    if needT:
                                        P.matmul(RB[:, ci, hf, 1, Lc], Amv[Lc, ci, hf, Lc], ATv[Lc, ci, hf, 1, Lc])
                        yield
                        if needA:
                            P.copy(Amv[Lc, 0:nc2, :, Lc], RA[:, 0:nc2, :, Lc], eng="act")
                        if needAt:
                            P.copy(ATv[Lc, 0:nc2, :, 0, Lc], RB[:, 0:nc2, :, 0, Lc], eng="act")
                        if needT:
                            P.tt(ATv[Lc, 0:nc2, :, 1, Lc], ATv[Lc, 0:nc2, :, 1, Lc], RB[:, 0:nc2, :, 1, Lc], ALU.add)
                        yield
                    for ci in range(nc2):
                        for hf in range(2):
                            P.matmul(h2[Lc, (ci * 2 + hf) * 128:(ci * 2 + hf + 1) * 128], ATv[Lc, ci, hf, 1, Lc], x0[Lc, ci, hf, :])
                    yield
                    x2v = h2[Lc, :].rearrange("p (c h a x) -> p c h a x", c=2, h=2, a=2)
                    x1v = x1[Lc, :, :, :].rearrange("p c a (h x) -> p c a h x", h=2)
                    P.copy(x1v[:, 0:nc2, 0, :, :], x2v[:, 0:nc2, :, 0, :], eng="act")
                    P.copy(x1v[:, 0:nc2, 1, :, :], x2v[:, 0:nc2, :, 1, :])
                    yield
                    GCp = h1[:, :].rearrange("p (c a x) -> p c a x", c=2, a=2)
                    RPp = h0[:, 256:512].rearrange("p (c x) -> p c x", c=2)
                    for ci in range(nc2):
                        P.matmul(GCp[:, ci, 0, :], x1[Lc, ci, 0, :], tk[Lc, ci, 0, :])
                        P.matmul(GCp[:, ci, 1, :], tk[Lc, ci, 0, :], x1[Lc, ci, 1, :], start=True, stop=False)
                        P.matmul(GCp[:, ci, 1, :], tk[Lc, ci, 1, :], tk[Lc, ci, 2, :], start=False, stop=True)
                    if mode != "prefix":
                        for ci in range(nc2):
                            P.matmul(RPp[:, ci, 0:2 * L], x1[Lc, ci, 0, :], sm[Lc, ci, :, 0, L:2 * L])
                    yield
                    PLv = Pm_[:, 0:n].rearrange("p (c t) -> p c t", t=L)[:, hv * 2:hv * 2 + nc2, L - 1:L]
                    for hf in range(2):
                        pr = slice(hf * 64, (hf + 1) * 64)
                        dg = gbd[pr, 0:nc2, hf * 64:(hf + 1) * 64]
                        P.tt(dg, ident2[pr, :].unsq(1).bc([64, nc2, 64]), PLv[pr].bc([64, nc2, 64]), ALU.mult, eng="pool")
                        P.tt(dg, dg, GCp[pr, 0:nc2, 0, hf * 64:(hf + 1) * 64], ALU.add)
                        P.copy(cst[pr, 0:nc2, :], GCp[pr, 0:nc2, 1, hf * 64:(hf + 1) * 64], eng="act")
                        if mode != "prefix":
                            P.tt(rp[pr, 0:nc2, Lc], RPp[pr, 0:nc2, hf * L:(hf + 1) * L],
                                 ARv[pr, hv * 2:hv * 2 + nc2, 1, :], ALU.add)
                    yield
                    Yt = h0[:, 0:256].rearrange("p (c x) -> p c x", c=2)
                    if mode != "prefix":
                        for ci in range(nc2):
                            P.matmul(Yt[:, ci, 0:2 * L], x1[Lc, ci, 1, :], sm[Lc, ci, :, 0, L:2 * L],
                                     start=(ci == 0), stop=False, skip_group_check=True)
                            P.matmul(Yt[:, ci, 0:2 * L], tk[Lc, ci, 2, :], sm[Lc, ci, :, 1, L:2 * L],
                                     start=False, stop=False, skip_group_check=True)
                    yield
                    for ci in range(nc2):
                        Hc = H[ft][hpar[ft]]
                        Hn = H[ft][1 - hpar[ft]]
                        P.matmul(GB[:, 0:64], gbd[:, ci, :], Hc)
                        P.tt(Hn, GB[:, 0:64], cst[:, ci, :], ALU.add)
                        if mode != "prefix":
                            for hf in range(2):
                                pr = slice(hf * 64, (hf + 1) * 64)
                                fx = (lambda v: v.bitcast(F32)) if hf else (lambda v: v)
                                P.matmul(Yt[pr, ci, hf * L:(hf + 1) * L], fx(Hc[pr, :]), fx(rp[pr, ci, Lc]), start=False, stop=True,
                                         skip_group_check=True)
                        hpar[ft] = 1 - hpar[ft]
                        yield
                    if mode != "prefix":
                        for hf in range(2):
                            pr = slice(hf * 64, (hf + 1) * 64)
                            P.copy(yT[pr, hv * 2 * L:(hv * 2 + nc2) * L].rearrange("p (c t) -> p c t", t=L),
                                   Yt[pr, 0:nc2, hf * L:(hf + 1) * L], eng=("act" if hf else "dve"))
                    yield

                gens = [half(hv) for hv in range(nhv)]
                if nhv == 2 and not SEQ_HALVES:
                    gA, gB = gens
                    aliveA = aliveB = True
                    stepA = 0
                    while aliveA or aliveB:
                        if aliveA:
                            try:
                                next(gA)
                            except StopIteration:
                                aliveA = False
                        if aliveB and (stepA >= 1 or not aliveA):
                            try:
                                next(gB)
                            except StopIteration:
                                aliveB = False
                        stepA += 1
                        pull(gnf, 3)
                        pull(filler, 2)
                else:
                    for g_ in gens:
                        for _ in g_:
                            pull(gnf, 3)
                            pull(filler, 2)

            def GN(ft):
                bonus_ = _bufs(ft)[6]
                pp = lambda i: pvec[:, i, ft:ft + 1]
                gb = GT.bitcast(F32)
                sq_ = rden[:, 0:n]
                rn_ = rden[:, 256:256 + n]
                yv = yT[:, 0:n]
                P.matmul(gb[:, 256:256 + n], bones64, yv)
                P.tt(yv, yv, gb[:, 256:256 + n], ALU.subtract)
                yield
                P.act(sq_, yv, AF.Square)
                yield
                P.matmul(gb[:, 256:256 + n], bones64, sq_)
                P.ts(rn_, gb[:, 256:256 + n], GN_EPS, ALU.add)
                yield
                P.act(rn_, rn_, AF.Ln)
                yield
                P.act(rn_, rn_, AF.Exp, scale=-0.5)
                yield
                P.tt(yv, yv, rn_, ALU.mult)
                yield
                P.ts(yv, yv, pp(4), ALU.mult, pp(5), ALU.add)
                yield
                P.tt(yv, yv, bonus_[:, 0:n], ALU.add)
                yield
                P.matmul(gb[:, 256:256 + n], g2[:, ft * 128:(ft + 1) * 128], sgd[:, 0:n])
                P.tt(mixT[:, 4 + ft, 0:n], yv, gb[:, 256:256 + n], ALU.mult)
                yield

            def ATT():
                npair = ntile
                for pi in range(npair):
                    nq = min(128, n - pi * 128)
                    nks = [128, nq]
                    qc = slice(pi * 128, pi * 128 + nq)
                    W = 4 * nq
                    pk3 = lambda v: v.rearrange("p (m q) -> p m q", m=4)
                    for g in range(2):
                        pr = slice(g * 64, (g + 1) * 64)
                        Sv = [B6, B7] if g == 0 else [GA0, GA1]
                        for kt in range(2):
                            nk = nks[kt]
                            k0 = pi * 128 + kt * 128
                            P.matmul(Sv[kt][0:nk, 0:W], kTh[pr, k0:k0 + nk], qT[pr, :, qc])
                            yield
                            s_v = s_sb[0:nk, kt, :, :].rearrange("p m q -> p (m q)")[:, 0:W]
                            p_v = pT[0:nk, kt, :, :].rearrange("p m q -> p (m q)")[:, 0:W]
                            P.stt(pk3(s_v), pk3(Sv[kt][0:nk, 0:W]),
                                  0.125, biasT[0:nk, kt, 4 * g:4 * g + 4, 0:nq], ALU.mult, ALU.add)
                            yield
                            if first_pair_mask and pi == 0 and kt == 0:
                                P.ts(s_v, s_v, negmask[0:nk, 0:1], ALU.add)
                                yield
                            P.act(p_v, s_v, AF.Exp)
                            yield
                        for kt in range(2):
                            nk = nks[kt]
                            p_v = pT[0:nk, kt, :, :].rearrange("p m q -> p (m q)")[:, 0:W]
                            P.matmul(B4[pr, 0:W], onesb[0:nk, 0:64], p_v, start=(kt == 0), stop=(kt == 1))
                            yield
                        for kt in range(2):
                            nk = nks[kt]
                            p_v = pT[0:nk, kt, :, :].rearrange("p m q -> p (m q)")[:, 0:W]
                            P.matmul(B5[pr, 0:W], vth[0:nk, pi + kt, pr], p_v, start=(kt == 0), stop=(kt == 1))
                            yield
                    rv = rden[:, 0:W]
                    P.act(rv, B4[:, 0:W], AF.Ln, bias=1.0)
                    yield
                    P.act(rv, rv, AF.Exp, scale=-1.0)
                    yield
                    P.tt(mixT[:, 0:4, qc], pk3(B5[:, 0:W]), pk3(rv), ALU.mult)
                    yield

            ecnt = [0]
            g0 = E(0)
            attg = ATT() if mode != "prefix" else None
            for _ in range(400):
                pull(g0, 2, True)
                pull(attg, 2, True)
            pull(g0, 10000, True)
            pull(attg, 10000, True)
            gn_prev = None
            for ft in range(4):
                nxt = E(ft + 1) if ft < 3 else None
                CG(ft, nxt, gn_prev)
                pull(gn_prev, 10000, True)
                pull(nxt, 10000, True)
                gn_prev = GN(ft) if mode != "prefix" else None
            pull(gn_prev, 10000, True)
            P.copy(zr[:, :, 0], zhist)
            if mode == "prefix":
                P.copy(kTh[:, 0:128], kTh[:, n:n + 128], eng="pool")
                P.copy(vth[:, 0, :], vth[:, 2, :], eng="pool")
                return
            if mode == "prompt":
                P.copy(kTh[:, 0:128], kTh[:, n:n + 128], eng="pool")
                P.copy(vth[:, 0, :], vth[:, 2, :], eng="pool")
            wo_banks = [[GA0, GA1], [B4, B5]]
            for t in range(ntile):
                tn = min(128, n - t * 128)
                for hh, bk in enumerate(wo_banks[t]):
                    for kc in range(8):
                        P.matmul(bk[0:tn, :], mixT[:, kc, t * 128:t * 128 + tn], w_out[:, kc, hh * 512:(hh + 1) * 512],
                                 start=(kc == 0), stop=(kc == 7))
            for t in range(ntile):
                tn = min(128, n - t * 128)
                for hh, bk in enumerate(wo_banks[t]):
                    P.tt(x1[t][0:tn, hh * 512:(hh + 1) * 512], bk[0:tn, :], xt[t][0:tn, hh * 512:(hh + 1) * 512], ALU.add)
                rmsnorm_T(x1[t], tn, 1, hcT, t * 128)
            for hd in range(4):
                gemm_fm(w_cq, hd * 128, hcT, n, lambda src, hd=hd: P.copy(qcT[:, hd, 0:n], src, eng="act"))
            sc = 128.0 ** -0.5
            SBk = [B4, B7]; DBk = [B5, GB]; PBk = [B6, GA0]
            pc2 = [pcT, pT.rearrange("p a m q -> p (a m q)")[:, 0:2 * NT].rearrange("p (a t) -> p a t", a=2)]
            rd2 = [rden, t1]

            def c_scores(hd):
                for mt in range(2):
                    P.matmul(SBk[hd % 2][:, mt * 256:mt * 256 + n], mkT[:, hd, mt * 128:(mt + 1) * 128], qcT[:, hd, 0:n])

            c_scores(0)
            for hd in range(4):
                if hd + 1 < 4:
                    c_scores(hd + 1)
                pcv = pc2[hd % 2]
                P.act(pcv[:, :, 0:n], SBk[hd % 2][:, :].rearrange("p (a t) -> p a t", a=2)[:, :, 0:n], AF.Exp, scale=sc)
                for mt in range(2):
                    P.matmul(DBk[hd % 2][:, 0:n], onesb, pcv[:, mt, 0:n], start=(mt == 0), stop=(mt == 1))
                for mt in range(2):
                    P.matmul(PBk[hd % 2][:, 0:n], mvt[:, mt, hd * 128:(hd + 1) * 128], pcv[:, mt, 0:n], start=(mt == 0), stop=(mt == 1))
                rdv = rd2[hd % 2]
                P.act(rdv[:, 0:n], DBk[hd % 2][:, 0:n], AF.Ln)
                P.act(rdv[:, 0:n], rdv[:, 0:n], AF.Exp, scale=-1.0)
                P.tt(ocT[:, hd, 0:n], PBk[hd % 2][:, 0:n], rdv[:, 0:n], ALU.mult)
            for t in range(ntile):
                tn = min(128, n - t * 128)
                for hh, bk in enumerate([GA0, GA1]):
                    for hd in range(4):
                        P.matmul(bk[0:tn, :], ocT[:, hd, t * 128:t * 128 + tn], w_co[:, hd, hh * 512:(hh + 1) * 512],
                                 start=(hd == 0), stop=(hd == 3))
                    P.tt(x1[t][0:tn, hh * 512:(hh + 1) * 512], bk[0:tn, :], x1[t][0:tn, hh * 512:(hh + 1) * 512], ALU.add)
                P.dma(D["x2s"][row0 + t * 128:row0 + t * 128 + tn, :], x1[t][0:tn, :])

        def emit_state_outputs(shname, wkvname, kname, vname, n_new):
            P.transpose(GB[0:14, 0:128], zhist, ident)
            P.copy(stg[0:14, 0:128], GB[0:14, 0:128])
            P.dma(D[shname], stg[0:14, 0:128], is_output=True)
            for ft in range(4):
                P.transpose(GB.bitcast(F32R)[0:64, 128:256], H[ft][hpar[ft]], identr)
                P.copy(mtmp[0:64, ft * 128:(ft + 1) * 128], GB[0:64, 128:256])
            P.dma(D[wkvname], mtmp[0:64, :].rearrange("p (h j) -> p h j", h=8), is_output=True)
            P.dma(D[kname][128 - n_new:128, :], kvtok[0:n_new, 0:128], is_output=True)
            P.dma(D[vname][128 - n_new:128, :], kvtok[0:n_new, 128:256], is_output=True)

        chk(2)
        P.memset(zr[:, :, 0], 0.0)
        P.memset(dtmp[:, 0:256], 0.0)
        for ft in range(4):
            P.copy(H[ft][0], dtmp[:, 0:64])
        for i in range(2):
            P.copy(Gbd_h[i].rearrange("p c x -> p (c x)"), dtmp[:, 0:256])
        P.memset(kTh[:, 0:128], 0.0)
        P.memset(vth[:, 0, :], 0.0)
        nblk = NTOK // NT
        for bi in range(nblk):
            block(D["xprev"][bi * NT:(bi + 1) * NT, :], NT, 64, "prefix")
        chk(3)
        for bi in range(nblk):
            block(D["xp"][bi * NT:(bi + 1) * NT, :], NT, 64, "prompt", first_pair_mask=(bi == 0), row0=bi * NT,
                  last=(bi == nblk - 1))
        emit_state_outputs("psh", "pwkv", "pk", "pv", 128)

        chk(4)
        P.dma(zhist, D["sshT"])
        P.copy(zr[:, :, 0], zhist)
        P.dma(mtmp[0:64, :].rearrange("p (h j) -> p h j", h=8), D["swkv"].rearrange("h i j -> i h j"))
        for ft in range(4):
            P.transpose(GB[:, 0:64], mtmp[0:64, ft * 128:(ft + 1) * 128], ident[0:64, 0:64])
            P.copy(H[ft][hpar[ft]], GB[:, 0:64])
        P.dma(stg[:, 0:128], D["ck"]); P.dma(stg[:, 128:256], D["cv"])
        P.transpose(GB[:, 0:128], stg[:, 0:128], ident)
        P.copy(kTh[:, 0:128], GB[:, 0:128])
        P.copy(vth[:, 0, :], stg[:, 128:256])
        for mt in range(2):
            P.dma(mtmp, D["cmk"][mt * 128:(mt + 1) * 128, :])
            for hd in range(4):
                P.transpose(GB[:, 128:256], mtmp[:, hd * 128:(hd + 1) * 128], ident)
                P.copy(mkT[:, hd, mt * 128:(mt + 1) * 128], GB[:, 128:256])
            P.dma(stg, D["cmv"][mt * 128:(mt + 1) * 128, :])
            P.copy(mvt[:, mt, :], stg)
        block(D["xs"], 32, 32, "sample", row0=NTOK, last=True)
        emit_state_outputs("ssh", "swkvo", "sk", "sv", 32)
        P.dma(D["sk"][0:96, :], D["ck"][32:128, :], is_output=True)
        P.dma(D["sv"][0:96, :], D["cv"][32:128, :], is_output=True)

        chk(5)
        while bg:
            o_, i_ = bg.pop(0)
            P.dma(o_, i_, eng="pool")
        P.barrier()
        ph1.close()
        w_up = sb("w_up", [128, 8, 4096], BF16)
        srcu = D["wupb"].rearrange("(k p) n -> p k n", p=128)
        for k in range(8):
            P.dma(w_up[:, k, :], srcu[:, k, :], eng=("sp" if k % 2 == 0 else "act"))
        w_dn = sb("w_dn", [128, 32, 1024], BF16)
        srcd = D["wdnb"].rearrange("(k p) n -> p k n", p=128)
        for k in range(8):
            P.dma(w_dn[:, 4 * k:4 * k + 4, :], srcd[:, 4 * k:4 * k + 4, :], eng=("sp" if k % 2 == 0 else "act"))
        gfin = sb("gfin", [128, 1024]); P.dma(gfin, V(D["gfin"].ap[0:1, :].partition_broadcast(128), D["gfin"].buf))
        x2s_ = [[sb("x2_%d_%d" % (s_, i), [128, 1024]) for i in range(2)] for s_ in range(2)]
        hmT2 = [sb("hmT%d" % s_, [128, 8, NT], BF16) for s_ in range(2)]
        ss3 = sb("ss3", [128, 1]); rs3 = sb("rs3", [128, 1]); sq3 = sb("sq3", [128, 1024])
        h2T = sb("h2T", [128, 32, NT], BF16)
        rl = sb("rl", [128, NT], BF16)
        sq2 = sb("sq2", [128, 1024]); xsc2 = sb("xsc2", [128, 1024], BF16)
        yo = [sb("yo%d" % i, [128, 1024]) for i in range(2)]
        ss2 = sb("ss2", [128, 1]); rs2 = sb("rs2", [128, 1])

        def rmsnorm_T2(xtile, n, outT, c0):
            P.memset(ss2[0:n, :], 0.0)
            P.act(sq2[0:n, :], xtile[0:n, :], AF.Square, accum_out=ss2[0:n, :])
            P.ts(ss2[0:n, :], ss2[0:n, :], 1.0 / 1024.0, ALU.mult, NORM_EPS, ALU.add)
            P.act(ss2[0:n, :], ss2[0:n, :], AF.Ln)
            P.act(rs2[0:n, :], ss2[0:n, :], AF.Exp, scale=-0.5)
            P.act(xsc2[0:n, :], xtile[0:n, :], AF.Copy, scale=rs2[0:n, :])
            for k in range(8):
                P.transpose(GT[:, k * 128:k * 128 + n], xsc2[0:n, k * 128:(k + 1) * 128], identb[0:n, 0:n])
            P.tt(outT[:, :, c0:c0 + n], GT.rearrange("p (k t) -> p k t", k=8)[:, :, 0:n],
                 gvec[:, 3, :].unsq(2).bc([128, 8, n]), ALU.mult)

        def mlp_prep(row0, n, slot):
            ntile = (n + 127) // 128
            for t in range(ntile):
                tn = min(128, n - t * 128)
                P.dma(x2s_[slot][t][0:tn, :], D["x2s"][row0 + t * 128:row0 + t * 128 + tn, :])
                rmsnorm_T2(x2s_[slot][t], tn, hmT2[slot], t * 128)

        def mlp_up(n, slot):
            for f in range(32):
                bk = gbank()
                for k in range(8):
                    P.matmul(bk[:, 0:n], w_up[:, k, f * 128:(f + 1) * 128], hmT2[slot][:, k, 0:n], start=(k == 0), stop=(k == 7))
                P.act(rl[:, 0:n], bk[:, 0:n], AF.Relu)
                P.tt(h2T[:, f, 0:n], rl[:, 0:n], rl[:, 0:n], ALU.mult)

        def mlp_down(n, slot, outname, orow0):
            ntile = (n + 127) // 128
            x2 = x2s_[slot]
            for t in range(ntile):
                tn = min(128, n - t * 128)
                for hh, bk in enumerate([B4, B5]):
                    for f in range(32):
                        P.matmul(bk[0:tn, :], h2T[:, f, t * 128:t * 128 + tn], w_dn[:, f, hh * 512:(hh + 1) * 512],
                                 start=(f == 0), stop=(f == 31))
                    P.tt(x2[t][0:tn, hh * 512:(hh + 1) * 512], bk[0:tn, :], x2[t][0:tn, hh * 512:(hh + 1) * 512], ALU.add)
                P.memset(ss3[0:tn, :], 0.0)
                P.act(sq3[0:tn, :], x2[t][0:tn, :], AF.Square, accum_out=ss3[0:tn, :])
                P.ts(ss3[0:tn, :], ss3[0:tn, :], 1.0 / 1024.0, ALU.mult, NORM_EPS, ALU.add)
                P.act(ss3[0:tn, :], ss3[0:tn, :], AF.Ln)
                P.act(rs3[0:tn, :], ss3[0:tn, :], AF.Exp, scale=-0.5)
                P.stt(yo[t][0:tn, :], x2[t][0:tn, :], rs3[0:tn, 0:1], gfin[0:tn, :], ALU.mult, ALU.mult)
                P.dma(D[outname][orow0 + t * 128:orow0 + t * 128 + tn, :], yo[t][0:tn, :], is_output=True)

        jobs = [(bi * NT, NT, "yp", bi * NT) for bi in range(NTOK // NT)] + [(NTOK, 32, "ys", 0)]
        mlp_prep(jobs[0][0], jobs[0][1], 0)
        for ji, (row0, n, oname, orow0) in enumerate(jobs):
            slot = ji % 2
            mlp_up(n, slot)
            if ji + 1 < len(jobs):
                mlp_prep(jobs[ji + 1][0], jobs[ji + 1][1], 1 - slot)
            mlp_down(n, slot, oname, orow0)
        P.finalize()
    return nc


NORM_EPS = 1e-6
GN_EPS = 64e-5


def _t5_bucket(rel):
    half, max_exact = 16, 8
    n = np.abs(rel)
    large = max_exact + (np.log(np.maximum(n, 1).astype(np.float32) / max_exact)
                         / np.float32(math.log(128 / max_exact)) * (half - max_exact)).astype(np.int32)
    large = np.minimum(large, half - 1)
    return np.where(rel > 0, half, 0) + np.where(n < max_exact, n, large)


_CACHE = {}


def make_in_maps(A):
    f32 = np.float32
    w_in = A["w_in"][0]
    qperm = np.concatenate([np.concatenate([np.arange(m * 64, m * 64 + 64), np.arange((4 + m) * 64, (4 + m) * 64 + 64)])
                            for m in range(4)])
    w_in_l = np.ascontiguousarray(np.concatenate([w_in[:, qperm], w_in[:, 512:]], axis=1))
    fm = lambda v, k: np.ascontiguousarray(v.reshape(k, 128).T)
    gvec = np.stack([fm(A["norm_mix_g"][0], 8), fm(A["norm_cross_g"][0], 8), fm(A["norm_mem_g"][0], 8),
                     fm(A["norm_mlp_g"][0], 8)], axis=1).astype(f32)
    pvec = np.stack([fm(A["rwkv_w0"][0], 4), fm(A["rwkv_a0"][0], 4), fm(A["rwkv_k_k"][0], 4), fm(A["rwkv_k_a"][0], 4),
                     fm(A["rwkv_ln_w"][0], 4), fm(A["rwkv_ln_b"][0], 4), fm(A["rwkv_r_k"][0].reshape(-1), 4)], axis=1).astype(f32)
    shared = {
        "w_in": w_in_l, "w_out": A["w_out"][0], "w_cq": A["w_cq"][0], "w_mk": A["w_mk"][0], "w_mv": A["w_mv"][0],
        "w_co": A["w_co"][0], "w_up": A["w_up"][0], "w_down": A["w_down"][0],
        "w2a2": np.ascontiguousarray(np.concatenate([A["rwkv_w2"][0], A["rwkv_a2"][0]], axis=0)),
        "g2": A["rwkv_g2"][0], "gvec": gvec, "gfin": A["norm_final_g"].reshape(1, 1024),
        "mu": fm(A["rwkv_mu"][0], 14), "pvec": pvec,
        "sink": A["attn_sink"][0].reshape(8, 1), "table": A["rel_bias_table"],
    }
    ident = np.eye(128, dtype=f32)
    bones = np.zeros((128, 128), f32); bones[:64, :64] = 1; bones[64:, 64:] = 1
    s_ = np.arange(64)[:, None]; t_ = np.arange(64)[None, :]
    mar = np.stack([(s_ < t_), (s_ <= t_)], axis=1).astype(f32)
    mts = (np.arange(64)[:, None] > np.arange(64)[None, :]).astype(f32)
    rmask = np.ones((128, NT), f32); rmask[:, ::64] = 0
    maskadd = np.zeros((128, 2, 128), f32)
    for kt in range(2):
        for k in range(128):
            kc = (kt * 128 - 128 + k) // 64
            for qc in range(2):
                if not (qc - 2 <= kc <= qc):
                    maskadd[k, kt, qc * 64:(qc + 1) * 64] = NEG
    rel = np.arange(384) - 255
    oh = np.zeros((32, 384), f32)
    oh[_t5_bucket(rel.astype(np.int32)), np.arange(384)] = 1
    oh[:, 383] = 0
    consts = {"c_ident": ident, "c_J": np.ascontiguousarray(ident[::-1]), "c_bones": bones, "c_bones64": bones / 64.0,
              "c_ident2": np.concatenate([np.eye(64, dtype=f32)] * 2, axis=0), "c_mar": mar, "c_mts": mts,
              "c_rmask": rmask, "c_maskadd": maskadd, "c_oh": oh}
    shared.update(consts)
    shared = {k: np.ascontiguousarray(v, dtype=f32) for k, v in shared.items()}
    in_maps = []
    for c in range(8):
        b, h = c // 2, c % 2
        m = dict(shared)
        m["xp"] = np.ascontiguousarray(A["x_prompt"][b, h * NTOK:(h + 1) * NTOK])
        m["xprev"] = np.ascontiguousarray(A["x_prompt"][b, 0:NTOK]) if h == 1 else np.zeros((NTOK, 1024), f32)
        m["xs"] = np.ascontiguousarray(A["x_sample"][c])
        m["mem"] = np.ascontiguousarray(A["mem_prompt"][b])
        m["ck"] = np.ascontiguousarray(A["cache_attn_k"][0, c].reshape(128, 128))
        m["cv"] = np.ascontiguousarray(A["cache_attn_v"][0, c].reshape(128, 128))
        m["cmk"] = np.ascontiguousarray(A["cache_mem_k"][0, c].reshape(256, 512))
        m["cmv"] = np.ascontiguousarray(A["cache_mem_v"][0, c].reshape(256, 512))
        m["sshT"] = fm(A["state_shift"][0, c, 0], 14).astype(f32)
        m["swkv"] = np.ascontiguousarray(A["state_wkv"][0, c])
        m["negmask"] = np.full((128, 1), NEG if h == 0 else 0.0, f32)
        in_maps.append(m)
    return in_maps


def assemble(R):
    f32 = np.float32
    y_prompt = np.stack([np.concatenate([R[2 * b]["yp"], R[2 * b + 1]["yp"]], axis=0) for b in range(4)])
    y_sample = np.stack([R[c]["ys"] for c in range(8)])
    p_k = np.stack([R[2 * b + 1]["pk"].reshape(128, 2, 64) for b in range(4)])[None]
    p_v = np.stack([R[2 * b + 1]["pv"].reshape(128, 2, 64) for b in range(4)])[None]
    p_mk = np.stack([R[2 * b]["pmk"].reshape(256, 4, 128) for b in range(4)])[None]
    p_mv = np.stack([R[2 * b]["pmv"].reshape(256, 4, 128) for b in range(4)])[None]
    p_sh = np.stack([R[2 * b + 1]["psh"].reshape(1, 1792) for b in range(4)])[None]
    p_S = np.stack([np.transpose(R[2 * b + 1]["pwkv"], (1, 0, 2)) for b in range(4)])[None]
    s_k = np.stack([R[c]["sk"].reshape(128, 2, 64) for c in range(8)])[None]
    s_v = np.stack([R[c]["sv"].reshape(128, 2, 64) for c in range(8)])[None]
    s_sh = np.stack([R[c]["ssh"].reshape(1, 1792) for c in range(8)])[None]
    s_S = np.stack([np.transpose(R[c]["swkvo"], (1, 0, 2)) for c in range(8)])[None]
    outs = (y_prompt, y_sample, p_k, p_v, p_mk, p_mv, p_sh, p_S, s_k, s_v, s_sh, s_S)
    return tuple(np.ascontiguousarray(o, dtype=f32) for o in outs)


def kernel(**inp):
    A = {k: np.asarray(v) for k, v in inp.items()}
    if "nc" not in _CACHE:
        nc = bass.Bass("TRN2", target_bir_lowering=False)
        build_program(nc)
        _CACHE["nc"] = nc
    nc = _CACHE["nc"]
    in_maps = make_in_maps(A)
    res = run_bass_kernel_spmd(nc, in_maps, core_ids=list(range(8)))
    return assemble(res.results)
```
